# Optimizing a Trainium2 kernel written in Bass

```python
import jax, jax.numpy as jnp
from jax import lax
import numpy as np

D_MODEL = 1024
BATCH = 8
SEQ = 2048
DEPTH = 2
DEC_BATCH = 128
DEC_SEQ = 4
PAST_LEN = 16384
PAGE_SIZE = 128

D_MIX = D_MODEL
D_DELTA = D_MIX // 2
N_DHEADS = 4
HEAD_DIM = D_DELTA // N_DHEADS
QK_CONV = 4
CHUNK = 64
D_SCONV = D_MIX // 4
SCONV_W = 3
D_CONF = D_MIX - D_DELTA - D_SCONV
CONF_W = 31
EPS = 1e-6
SPLIT_WIDTHS = (3 * D_DELTA, N_DHEADS, N_DHEADS, D_DELTA,
                D_SCONV, D_SCONV, D_SCONV, D_SCONV,
                2 * D_CONF, D_CONF)
SPLIT_POINTS = tuple(int(s) for s in np.cumsum(SPLIT_WIDTHS)[:-1])
D_IN = int(sum(SPLIT_WIDTHS))

kernel_name = "hymba_delta_shortconv_conformer_step"


def rmsnorm(x, g):
    xf = x.astype(jnp.float32)
    y = xf * lax.rsqrt(jnp.mean(xf * xf, axis=-1, keepdims=True) + EPS)
    return (y * g.astype(jnp.float32)).astype(x.dtype)


def l2norm(x):
    return x * lax.rsqrt(jnp.sum(x * x, axis=-1, keepdims=True) + EPS)


def causal_dwconv(x, buf, w):
    xp = jnp.concatenate([buf.astype(x.dtype), x], axis=1)
    y = lax.conv_general_dilated(xp, w.astype(x.dtype)[:, None, :], (1,), 'VALID',
                                 dimension_numbers=('NWC', 'WIO', 'NWC'),
                                 feature_group_count=x.shape[-1])
    return y, xp[:, -(w.shape[0] - 1):]


def _to_chunks(a, C, pad):
    a = jnp.pad(a, [(0, 0), (0, pad)] + [(0, 0)] * (a.ndim - 2))
    b, tp, h = a.shape[:3]
    a = a.reshape((b, tp // C, C, h) + a.shape[3:])
    return jnp.moveaxis(a, (1, 3), (0, 2))


def gated_delta_rule(q, k, v, g, beta, S0):
    T = q.shape[1]
    dk = q.shape[-1]
    C = min(CHUNK, T)
    pad = (-T) % C
    q = q * (dk ** -0.5)
    qc, kc, vc = _to_chunks(q, C, pad), _to_chunks(k, C, pad), _to_chunks(v, C, pad)
    gcum = jnp.cumsum(_to_chunks(g, C, pad), axis=-1)
    bc = _to_chunks(beta, C, pad)
    kb = kc * bc[..., None]
    vb = vc * bc[..., None]
    tril = jnp.tril(jnp.ones((C, C), dtype=bool))
    strict = jnp.tril(jnp.ones((C, C), dtype=bool), -1)
    diff = gcum[..., :, None] - gcum[..., None, :]
    decay = jnp.where(tril, jnp.exp(jnp.where(tril, diff, 0.0)), 0.0)
    L = jnp.where(strict, jnp.einsum('nbhid,nbhjd->nbhij', kb, kc) * decay, 0.0)
    eye = jnp.eye(C, dtype=L.dtype)
    Tinv = lax.linalg.triangular_solve(eye + L, jnp.broadcast_to(eye, L.shape),
                                       left_side=True, lower=True, unit_diagonal=True)
    u = jnp.einsum('nbhij,nbhje->nbhie', Tinv, vb)
    w = jnp.einsum('nbhij,nbhjd->nbhid', Tinv, kb * jnp.exp(gcum)[..., None])
    qk = jnp.where(tril, jnp.einsum('nbhid,nbhjd->nbhij', qc, kc) * decay, 0.0)

    def step(S, xs):
        q_i, k_i, u_i, w_i, g_i, qk_i = xs
        v_new = u_i - jnp.einsum('bhcd,bhde->bhce', w_i, S)
        o = (jnp.einsum('bhcd,bhde->bhce', q_i * jnp.exp(g_i)[..., None], S)
             + jnp.einsum('bhij,bhje->bhie', qk_i, v_new))
        g_last = g_i[..., -1]
        S = (S * jnp.exp(g_last)[..., None, None]
             + jnp.einsum('bhcd,bhce->bhde', k_i * jnp.exp(g_last[..., None] - g_i)[..., None], v_new))
        return S, o

    S, o = lax.scan(step, S0, (qc, kc, u, w, gcum, qk))
    b = q.shape[0]
    o = jnp.moveaxis(o, (0, 2), (1, 3)).reshape(b, -1, q.shape[2], v.shape[-1])[:, :T]
    return o, S


def mixer_layer(x, st_delta, st_qkv, st_sconv, st_cconv, norm_g, w_in, conv_qkv_w, a_log,
                dt_bias, delta_norm_g, sconv_w, cconv_w, cconv_b, cln_g, cln_b, w_out):
    b, t, _ = x.shape
    h = rmsnorm(x, norm_g)
    p = h @ w_in
    (qkv, b_log, a_in, gate_d, s_B, s_C, s_x, gate_s, glu_in, gate_c) = jnp.split(p, SPLIT_POINTS, axis=-1)

    qkv_c, new_qkv = causal_dwconv(qkv, st_qkv, conv_qkv_w)
    qkv_c = jax.nn.silu(qkv_c).astype(jnp.float32)
    q, k, v = jnp.split(qkv_c.reshape(b, t, 3, N_DHEADS, HEAD_DIM), 3, axis=2)
    q, k, v = l2norm(q[:, :, 0]), l2norm(k[:, :, 0]), v[:, :, 0]
    beta = jax.nn.sigmoid(b_log.astype(jnp.float32))
    g = -jnp.exp(a_log.astype(jnp.float32)) * jax.nn.softplus(
        a_in.astype(jnp.float32) + dt_bias.astype(jnp.float32))
    o_d, new_delta = gated_delta_rule(q, k, v, g, beta, st_delta.astype(jnp.float32))
    o_d = rmsnorm(o_d, delta_norm_g).reshape(b, t, D_DELTA).astype(x.dtype)
    o_d = o_d * jax.nn.silu(gate_d)

    hs = s_C * s_x
    ys, new_sconv = causal_dwconv(hs, st_sconv, sconv_w)
    o_s = s_B * ys * jax.nn.silu(gate_s)

    ga, gb = jnp.split(glu_in, 2, axis=-1)
    u = ga * jax.nn.sigmoid(gb)
    yc, new_cconv = causal_dwconv(u, st_cconv, cconv_w)
    yc = (yc + cconv_b).astype(jnp.float32)
    mu = jnp.mean(yc, axis=-1, keepdims=True)
    var = jnp.mean(jnp.square(yc - mu), axis=-1, keepdims=True)
    yc = ((yc - mu) * lax.rsqrt(var + EPS) * cln_g.astype(jnp.float32) + cln_b.astype(jnp.float32)).astype(x.dtype)
    o_c = jax.nn.silu(yc) * jax.nn.silu(gate_c)

    o = jnp.concatenate([o_d, o_s, o_c], axis=-1)
    y = x + o @ w_out
    return (y, new_delta.astype(st_delta.dtype), new_qkv.astype(st_qkv.dtype),
            new_sconv.astype(st_sconv.dtype), new_cconv.astype(st_cconv.dtype))


def setup_inputs(seed: int = 0) -> dict:
    key = jax.random.key(seed)
    ks = jax.random.split(key, 20)
    f32 = jnp.float32
    dt = jnp.exp(jax.random.uniform(ks[9], (DEPTH, N_DHEADS), f32, np.log(1e-3), np.log(1e-1)))
    return {
        "x_prompt": jax.random.normal(ks[0], (BATCH, SEQ, D_MODEL), f32),
        "x_sample": jax.random.normal(ks[1], (DEC_BATCH, DEC_SEQ, D_MODEL), f32),
        "state_delta": 0.3 * jax.random.normal(ks[2], (DEPTH, DEC_BATCH, N_DHEADS, HEAD_DIM, HEAD_DIM), f32),
        "state_qkv_conv": jax.random.normal(ks[3], (DEPTH, DEC_BATCH, QK_CONV - 1, 3 * D_DELTA), f32),
        "state_sconv": jax.random.normal(ks[4], (DEPTH, DEC_BATCH, SCONV_W - 1, D_SCONV), f32),
        "state_cconv": 0.5 * jax.random.normal(ks[5], (DEPTH, DEC_BATCH, CONF_W - 1, D_CONF), f32),
        "norm_g": 1.0 + 0.02 * jax.random.normal(ks[6], (DEPTH, D_MODEL), f32),
        "w_in": jax.random.normal(ks[7], (DEPTH, D_MODEL, D_IN), f32) * D_MODEL ** -0.5,
        "conv_qkv_w": jax.random.normal(ks[8], (DEPTH, QK_CONV, 3 * D_DELTA), f32) * QK_CONV ** -0.5,
        "a_log": jnp.log(jax.random.uniform(ks[10], (DEPTH, N_DHEADS), f32, 1.0, 16.0)),
        "dt_bias": dt + jnp.log(-jnp.expm1(-dt)),
        "delta_norm_g": 1.0 + 0.02 * jax.random.normal(ks[11], (DEPTH, HEAD_DIM), f32),
        "sconv_w": jax.random.normal(ks[12], (DEPTH, SCONV_W, D_SCONV), f32) * SCONV_W ** -0.5,
        "cconv_w": jax.random.normal(ks[13], (DEPTH, CONF_W, D_CONF), f32) * CONF_W ** -0.5,
        "cconv_b": 0.02 * jax.random.normal(ks[14], (DEPTH, D_CONF), f32),
        "cln_g": 1.0 + 0.02 * jax.random.normal(ks[15], (DEPTH, D_CONF), f32),
        "cln_b": 0.02 * jax.random.normal(ks[16], (DEPTH, D_CONF), f32),
        "w_out": jax.random.normal(ks[17], (DEPTH, D_MIX, D_MODEL), f32) * D_MIX ** -0.5,
        "final_norm_g": 1.0 + 0.02 * jax.random.normal(ks[18], (D_MODEL,), f32),
    }


def reference(x_prompt, x_sample, state_delta, state_qkv_conv, state_sconv, state_cconv,
              norm_g, w_in, conv_qkv_w, a_log, dt_bias, delta_norm_g, sconv_w, cconv_w,
              cconv_b, cln_g, cln_b, w_out, final_norm_g):
    bp = x_prompt.shape[0]
    dtp = x_prompt.dtype
    xp, xs = x_prompt, x_sample
    p_delta, p_qkv, p_sconv, p_cconv = [], [], [], []
    s_delta, s_qkv, s_sconv, s_cconv = [], [], [], []
    for l in range(DEPTH):
        w_l = (norm_g[l], w_in[l], conv_qkv_w[l], a_log[l], dt_bias[l], delta_norm_g[l],
               sconv_w[l], cconv_w[l], cconv_b[l], cln_g[l], cln_b[l], w_out[l])
        xp, d0, q0, sc0, cc0 = mixer_layer(
            xp,
            jnp.zeros((bp, N_DHEADS, HEAD_DIM, HEAD_DIM), dtp),
            jnp.zeros((bp, QK_CONV - 1, 3 * D_DELTA), dtp),
            jnp.zeros((bp, SCONV_W - 1, D_SCONV), dtp),
            jnp.zeros((bp, CONF_W - 1, D_CONF), dtp),
            *w_l)
        p_delta.append(d0); p_qkv.append(q0); p_sconv.append(sc0); p_cconv.append(cc0)
        xs, d1, q1, sc1, cc1 = mixer_layer(
            xs, state_delta[l], state_qkv_conv[l], state_sconv[l], state_cconv[l], *w_l)
        s_delta.append(d1); s_qkv.append(q1); s_sconv.append(sc1); s_cconv.append(cc1)
    y_prompt = rmsnorm(xp, final_norm_g)
    y_sample = rmsnorm(xs, final_norm_g)
    return (y_prompt, y_sample,
            jnp.stack(p_delta), jnp.stack(p_qkv), jnp.stack(p_sconv), jnp.stack(p_cconv),
            jnp.stack(s_delta), jnp.stack(s_qkv), jnp.stack(s_sconv), jnp.stack(s_cconv))
```

```python
import contextlib
import math
import numpy as np
import concourse.bass as bass
import concourse.mybir as mybir
from concourse.bass_utils import run_bass_kernel_spmd

F32 = mybir.dt.float32
BF16 = mybir.dt.bfloat16
AF = mybir.ActivationFunctionType
ALU = mybir.AluOpType
AX = mybir.AxisListType

NCORES = 8
L = 2
D = 1024
SEQ = 2048
NSEQ_S = 16
TS = 4
NTOK = SEQ + NSEQ_S * TS
EPS = 1e-6
NEG = -30000.0
NGROUPS = 2
NFILL = 0
CSTEPS = 5
CLEAD = 3
NSLOT = 10
ESTEPS = 1

C_QKV, C_GD, C_SB, C_SC, C_SX, C_GS, C_GA, C_GB, C_GC = 0, 1536, 2048, 2304, 2560, 2816, 3072, 3328, 3584
NCOL = 3840

P_NG, P_CQ, P_CS, P_CC, P_CB, P_CLG, P_CLB, P_DNG, P_ALOG, P_DTB = 0, 8, 56, 62, 124, 126, 128, 130, 131, 135
P_PER = 139
P_FNG = L * P_PER
NP_ = P_FNG + D

F_ID, F_TRIP, F_TRIS, F_LMP, F_LMS, F_ONES, F_SEL, F_LASTP, F_LASTS, F_SEQM, F_MHALF, F_LN05, F_NLS = (
    0, 128, 256, 384, 512, 640, 768, 1152, 1153, 1169, 1185, 1186, 1187)
F_EPS = 1188
NCF = 1192
B_ID, B_NSP, B_NSS, B_NIP, B_NIS, B_D16, B_L32, B_L64, B_L128, B_ONES = (
    0, 128, 256, 384, 512, 640, 768, 896, 1024, 1152)
B_SELNB, B_SELC = 1280, 1408
NCB = 1536


class Res:
    __slots__ = ("name", "w", "rs")

    def __init__(self, name):
        self.name = name
        self.w = None
        self.rs = []


class T:
    __slots__ = ("h", "res", "psum")

    def __init__(self, h, name, psum=False):
        self.h = h
        self.res = Res(name)
        self.psum = psum

    def __getitem__(self, k):
        return self.h[k]


class G4:
    __slots__ = ("h", "t")

    def __init__(self, h, name):
        self.h = h
        self.t = [T(h, "%s_%d" % (name, i)) for i in range(4)]

    def __getitem__(self, k):
        return self.h[k]

    def g(self, hs):
        return [self.t[h] for h in hs]


def _flat(xs):
    out = []
    for x in xs:
        if isinstance(x, G4):
            out.extend(x.t)
        elif isinstance(x, (list, tuple)):
            out.extend(_flat(x))
        else:
            out.append(x)
    return out


class Instr:
    __slots__ = ("eng", "meth", "kw", "deps", "dma", "sig", "val", "semkey", "fill")

    def __init__(self, eng, meth, kw, dma, semkey):
        self.eng = eng
        self.meth = meth
        self.kw = kw
        self.deps = []
        self.dma = dma
        self.sig = dma
        self.val = None
        self.semkey = semkey
        self.fill = 0


class Prog:
    ENG = ("pe", "act", "dve", "pool", "sp")

    def __init__(self, nc):
        self.nc = nc
        self.lists = {e: [] for e in self.ENG}
        self.fill = 0
        self.fill_kw = None

    def op(self, eng, meth, kw, r=(), w=(), dma=False, semkey=None):
        ins = Instr(eng, meth, kw, dma, semkey)
        if eng == "pe":
            ins.fill = self.fill
        r = _flat(r)
        w = _flat(w)
        w = list(w) + [t for t in r if isinstance(t, T) and t.psum]
        deps = []
        for t in r:
            x = t.res if isinstance(t, T) else t
            if x.w is not None:
                deps.append(x.w)
        for t in w:
            x = t.res if isinstance(t, T) else t
            if x.w is not None:
                deps.append(x.w)
            last_by_eng = {}
            for rd in x.rs:
                if rd.dma or rd.eng in ("pool", "sp"):
                    deps.append(rd)
                else:
                    last_by_eng[rd.eng] = rd
            deps.extend(last_by_eng.values())
        seen = set()
        for d in deps:
            if id(d) in seen or d is ins:
                continue
            seen.add(id(d))
            if d.eng == "pe" and eng == "pe" and not d.dma and not dma:
                continue
            ins.deps.append(d)
            d.sig = True
        for t in r:
            x = t.res if isinstance(t, T) else t
            x.rs.append(ins)
        for t in w:
            x = t.res if isinstance(t, T) else t
            x.w = ins
            x.rs = []
        self.lists[eng].append(ins)
        return ins

    def emit(self):
        nc = self.nc
        pos = {}
        for e in self.ENG:
            for i, ins in enumerate(self.lists[e]):
                pos[id(ins)] = i
                if not ins.dma:
                    ins.sig = False
        for e in self.ENG:
            waited_idx = {}
            for ins in self.lists[e]:
                best = {}
                keep = []
                for d in ins.deps:
                    if d.dma:
                        keep.append(d)
                        continue
                    if waited_idx.get(d.eng, -1) >= pos[id(d)]:
                        continue
                    if d.eng not in best or pos[id(best[d.eng])] < pos[id(d)]:
                        best[d.eng] = d
                for pe_, d in best.items():
                    waited_idx[pe_] = pos[id(d)]
                    d.sig = True
                    keep.append(d)
                ins.deps = keep
        semkeys = []
        counts = {}
        for e in self.ENG:
            for ins in self.lists[e]:
                if not ins.sig:
                    continue
                k = ("dma", ins.semkey) if ins.dma else ("eng", e)
                if k not in counts:
                    counts[k] = 0
                    semkeys.append(k)
                counts[k] += 16 if ins.dma else 1
                ins.val = (k, counts[k])
        assert len(semkeys) <= 140, len(semkeys)
        with contextlib.ExitStack() as st:
            sems = {}
            for i, k in enumerate(semkeys):
                sems[k] = st.enter_context(nc.semaphore("s%d" % i))
            block = st.enter_context(nc.Block())

            def mk(e):
                def body(engine):
                    waited = {}
                    for ins in self.lists[e]:
                        need = [d.val for d in ins.deps if waited.get(d.val[0], 0) < d.val[1]]
                        if need and ins.fill and self.fill_kw is not None:
                            for _ in range(ins.fill):
                                engine.matmul(**self.fill_kw)
                        for d in ins.deps:
                            k, v = d.val
                            if waited.get(k, 0) < v:
                                engine.wait_ge(sems[k], v)
                                waited[k] = v
                        h = getattr(engine, ins.meth)(**ins.kw)
                        if ins.sig:
                            k, v = ins.val
                            h.then_inc(sems[k], 16 if ins.dma else 1)
                    if e == "sp":
                        for k, v in counts.items():
                            if k[0] == "dma" and waited.get(k, 0) < v:
                                engine.wait_ge(sems[k], v)
                return body
            block.tensor(mk("pe"))
            block.scalar(mk("act"))
            block.vector(mk("dve"))
            block.gpsimd(mk("pool"))
            block.sync(mk("sp"))
        return len(semkeys)


def make_consts():
    cf = np.zeros((128, NCF), np.float32)
    i = np.arange(128)
    cf[:, F_ID:F_ID + 128] = np.eye(128)
    cf[:, F_TRIP:F_TRIP + 128] = (i[:, None] <= i[None, :])
    s64 = np.arange(64)
    same = (s64[:, None] // TS) == (s64[None, :] // TS)
    cf[:64, F_TRIS:F_TRIS + 64] = same & (s64[:, None] <= s64[None, :])
    cf[:, F_LMP:F_LMP + 128] = (i[:, None] == 127)
    cf[:64, F_LMS:F_LMS + 64] = (s64[:, None] == (s64[None, :] // TS) * TS + TS - 1)
    cf[:, F_ONES:F_ONES + 128] = 1.0
    for r in range(3):
        cf[r, F_SEL + r * 128:F_SEL + (r + 1) * 128] = 1.0
    cf[127, F_LASTP] = 1.0
    for q in range(NSEQ_S):
        cf[TS * q + TS - 1, F_LASTS + q] = 1.0
        cf[TS * q:TS * q + TS, F_SEQM + q] = 1.0
    cf[:, F_MHALF] = -0.5
    cf[:, F_LN05] = math.log(0.5)
    cf[:, F_NLS] = -math.log(math.sqrt(128.0))
    cf[:, F_EPS] = EPS
    cb = np.zeros((128, NCB), np.float32)
    cb[:, B_ID:B_ID + 128] = np.eye(128)
    cb[:, B_NSP:B_NSP + 128] = np.where(i[:, None] > i[None, :], 0.0, NEG)
    cb[:, B_NSS:B_NSS + 128] = NEG
    cb[:64, B_NSS:B_NSS + 64] = np.where(same & (s64[:, None] > s64[None, :]), 0.0, NEG)
    cb[:, B_NIP:B_NIP + 128] = np.where(i[None, :] >= i[:, None], 0.0, NEG)
    cb[:, B_NIS:B_NIS + 128] = NEG
    cb[:64, B_NIS:B_NIS + 64] = np.where(same & (s64[None, :] >= s64[:, None]), 0.0, NEG)
    cb[:, B_D16:B_D16 + 128] = (i[:, None] // 16 == i[None, :] // 16)
    cb[:, B_L32:B_L32 + 128] = (i[:, None] // 32 == i[None, :] // 32) & (i[:, None] // 16 != i[None, :] // 16)
    cb[:, B_L64:B_L64 + 128] = (i[:, None] // 64 == i[None, :] // 64) & (i[:, None] // 32 != i[None, :] // 32)
    cb[:, B_L128:B_L128 + 128] = (i[:, None] // 64 != i[None, :] // 64)
    cb[:, B_ONES:B_ONES + 128] = 1.0
    for r_ in (1, 4, 7):
        cb[r_, B_SELNB:B_SELNB + 128] = 1.0
    for r_ in (2, 5, 8):
        cb[r_, B_SELC:B_SELC + 128] = 1.0
    return cf, cb


def pack_params(inp):
    p = np.zeros((128, NP_), np.float32)
    for l in range(L):
        o = l * P_PER
        p[:, o + P_NG:o + P_NG + 8] = inp["norm_g"][l].reshape(8, 128).T
        cq = inp["conv_qkv_w"][l].reshape(4, 12, 128)
        p[:, o + P_CQ:o + P_CQ + 48] = cq.transpose(2, 1, 0).reshape(128, 48)
        cs = inp["sconv_w"][l].reshape(3, 2, 128)
        p[:, o + P_CS:o + P_CS + 6] = cs.transpose(2, 1, 0).reshape(128, 6)
        cc = inp["cconv_w"][l].reshape(31, 2, 128)
        p[:, o + P_CC:o + P_CC + 62] = cc.transpose(2, 1, 0).reshape(128, 62)
        p[:, o + P_CB:o + P_CB + 2] = inp["cconv_b"][l].reshape(2, 128).T
        p[:, o + P_CLG:o + P_CLG + 2] = inp["cln_g"][l].reshape(2, 128).T
        p[:, o + P_CLB:o + P_CLB + 2] = inp["cln_b"][l].reshape(2, 128).T
        p[:, o + P_DNG] = inp["delta_norm_g"][l]
        p[:, o + P_ALOG:o + P_ALOG + 4] = inp["a_log"][l][None, :]
        p[:, o + P_DTB:o + P_DTB + 4] = inp["dt_bias"][l][None, :]
    p[:, P_FNG:P_FNG + D] = inp["final_norm_g"][None, :]
    return p


def build_nc(nlayers=L, ntiles=5):
    nc = bass.Bass("TRN2", target_bir_lowering=False)

    def din(name, shape):
        return nc.dram_tensor(name, list(shape), F32, kind="ExternalInput").ap()

    def dout(name, shape):
        return nc.dram_tensor(name, list(shape), F32, kind="ExternalOutput").ap()

    xin = din("xin", [NTOK, D])
    w_in = din("w_in", [L, D, NCOL])
    w_bd = din("w_bd", [L, D, 8])
    w_out = din("w_out", [L, D, D])
    prm_d = din("prm", [128, NP_])
    cf_d = din("cf", [128, NCF])
    cb_d = din("cb", [128, NCB])
    st_delta = din("st_delta", [L, NSEQ_S, 4, 128, 128])
    st_qkv = din("st_qkv", [L, NSEQ_S * 3, 1536])
    st_sc = din("st_sc", [L, NSEQ_S * 2, 256])
    st_cc = din("st_cc", [L, NSEQ_S * 30, 256])
    y1 = nc.dram_tensor("y1", [NTOK, D], F32, kind="Internal").ap()
    yout = dout("yout", [NTOK, D])
    p_delta = dout("p_delta", [L, 4, 128, 128])
    p_qkv = dout("p_qkv", [L, 3, 1536])
    p_sconv = dout("p_sconv", [L, 2, 256])
    p_cconv = dout("p_cconv", [L, 30, 256])
    s_delta = dout("s_delta", [L, NSEQ_S, 4, 128, 128])
    s_qkv = dout("s_qkv", [L, NSEQ_S * 3, 1536])
    s_sconv = dout("s_sconv", [L, NSEQ_S * 2, 256])
    s_cconv = dout("s_cconv", [L, NSEQ_S * 30, 256])

    st = contextlib.ExitStack()
    with st:
        def sb4(name, shape, dt=F32):
            return G4(st.enter_context(nc.sbuf_tensor("sb_" + name, list(shape), dt)), name)

        evc = [0]

        def ev():
            evc[0] += 1
            return "act" if evc[0] % 2 else "dve"

        def sb(name, shape, dt=F32):
            return T(st.enter_context(nc.sbuf_tensor("sb_" + name, list(shape), dt)), name)

        P = Prog(nc)
        op = P.op

        Wg = [sb("wg%d" % g, [128, 8, n], BF16) for g, n in enumerate([1536, 512, 1024, 768])]
        Wgoff = [0, 1536, 2048, 3072]
        Wbd = sb("wbd", [128, 8, 8], BF16)
        Wo = sb("wo", [128, 8, D], BF16)
        prm = sb("prm", [128, NP_])
        cf = sb("cf", [128, NCF])
        cb = sb("cb", [128, NCB], BF16)
        dcr = [sb("dcr%d" % i, [128, 128], BF16) for i in range(NSLOT)]
        dcc = [0]
        xb = [sb("xb%d" % i, [128, D]) for i in range(2)]
        xh = sb("xh", [128, D], BF16)
        junk = xh
        hT = sb("hT", [128, 8, 512], BF16)
        oT = sb4("oT", [128, 8, 512], BF16)
        rr = [sb("rr%d" % i, [128, 528], BF16) for i in range(3)]
        raw_s = sb("raw_s", [128, 12, NSEQ_S, 7], BF16)
        rawlast = sb("rawlast", [128, 12, 4])
        hist = sb("hist", [128, 12, 4], BF16)
        qkvT = sb("qkvT", [128, 12, 512], BF16)
        gd2 = sb("gd2", [128, 4, 512], BF16)
        sqs = sb("sqs", [128, 512], BF16)
        sqs2 = sb("sqs2", [128, 512], BF16)
        hsb = sb("hsb", [128, 2, 520])
        ub = sb("ub", [128, 2, 544])
        ubf = sb("ubf", [128, 2, 544], BF16)
        stg = xb[0]
        stg2 = xb[1]
        S = sb4("S", [128, 512])
        Sbf = sb4("Sbf", [128, 512], BF16)
        ss = sb("ss", [128, 8])
        rs = sb("rs", [128, 8])
        negA = sb("negA", [128, L * 4])
        hdng = sb("hdng", [128, L])
        bdS = sb("bdS", [128, 4, 8])
        zz = sb("zz", [128, 4, 8])
        az = sb("az", [128, 4, 8])
        lnin = sb("lnin", [128, 4, 16])
        lnout = sb("lnout", [128, 4, 16])
        spz = sb("spz", [128, 4, 8])
        gg = sb("gg", [128, 4, 4])
        gc = sb("gc", [128, 4, 4])
        hk = sb("hk", [128, 4, 4])
        tm1 = sb("tm1", [128, 4, 4])
        tm2 = sb("tm2", [128, 4, 4])
        R3 = sb("R3", [128, 4, 4, 3])
        ea = sb("ea", [128, 4, 4])
        evb = sb("evb", [128, 4, 4])
        ec = sb("ec", [128, 4, 4])
        es = sb("es", [128, 4, 4])
        glt = sb("glt", [128, 4, 4])
        gsel = sb("gsel", [128, 64])
        dec = sb("dec", [128, 64])
        Rw = sb4("Rw0", [9, 512], BF16)
        R9 = sb("R9", [128, 4, 4, 9], BF16)
        R3r = sb("R3r", [128, 4, 4, 3])
        ssqo = sb4("ssqo", [128, 4])
        rso = sb4("rso", [128, 4])
        lst = sb("lst", [128, 4, 2])
        lvr = sb("lvr", [128, 4])
        LP = sb("LP", [128, 4, 2])
        hcl = sb("hcl", [128, L * 4])
        hcc = sb("hcc", [128, L * 62])
        ssf = sb("ssf", [128, 2])
        rsf = sb("rsf", [128, 2])
        EA = sb4("EA", [128, 512])
        EM = sb4("EM", [128, 512])
        t1 = sb4("t1", [128, 512])
        ot = sb4("ot", [128, 512])
        osq = sb4("osq", [128, 512])
        o1T = ot
        bfn = ["Am", "Bm", "Mt", "Kw", "Ks", "Vb", "Pa", "Pb", "Qa", "Qb", "Ta", "Tb", "Ua", "Ub", "Xm", "Ao", "th", "th2", "Mt2", "Ks2", "Vb2"]
        thc = [0]
        bt = {n: sb4(n, [128, 512], BF16) for n in bfn}
        bt["Bo"] = bt["Ao"]
        bt["nwT"] = bt["Qb"]
        bt["on"] = bt["Qa"]
        bt["vnew"] = bt["Xm"]
        bt["vnT"] = bt["Pa"]
        bt["Ksm"] = bt["Pb"]
        S0 = [EA, EM]
        SN = [t1, osq]
        S0b = [bt["Ao"], bt["Pa"], bt["Pb"]]
        MtB = [bt["Mt"], bt["Mt2"]]
        KsB = [bt["Ks"], bt["Ks2"]]
        VbB = [bt["Vb"], bt["Vb2"]]
        mu, var = ot, osq
        csb = sb("csb", [128, 512])
        ys = sb("ys", [128, 512])
        g2 = sb("g2", [128, 512])
        LR = csb
        yc = sb("yc", [128, 2, 512])
        ysq = [sb("ysq0", [128, 512], BF16), sb("ysq1", [128, 512], BF16)]
        rawf_s = yc

        banks = [T(st.enter_context(nc.psum_tensor("pb%d" % i, [128, 512], F32)), "pb%d" % i, psum=True) for i in range(8)]
        bpool = {"all": list(range(7)), "D": [0, 1, 2, 3], "C": [4, 5, 6]}
        P.fill_kw = dict(out=banks[7][:, 0:128], lhsT=cb[:, B_ID:B_ID + 128], rhs=cb[:, B_ID:B_ID + 128], start=True, stop=True)
        bcur = ["all"]
        bidx = {"all": 0, "D": 0, "C": 0}

        def bank():
            p_ = bcur[0]
            lst_ = bpool[p_]
            b = banks[lst_[bidx[p_] % len(lst_)]]
            bidx[p_] += 1
            return b

        def bfv(b):
            return b.h[:, :].bitcast(BF16)

        y1res = [Res("y1_%d" % i) for i in range(17)]

        def pc(o, n=1):
            return prm[:, o:o + n]

        op("sp", "dma_start", dict(out=prm[:, :], in_=prm_d), w=[prm], dma=True, semkey="prm")
        op("sp", "dma_start", dict(out=cf[:, :], in_=cf_d), w=[cf], dma=True, semkey="cf")
        op("pool", "dma_start", dict(out=cb[:, :], in_=cb_d), w=[cb], dma=True, semkey="cb")
        identf = cf[:, F_ID:F_ID + 128]
        identb = cb[:, B_ID:B_ID + 128]

        def load_wgroup(l, g):
            if g < 4:
                n = Wg[g].h.shape[2]
                src = w_in[l, :, Wgoff[g]:Wgoff[g] + n].rearrange("(k p) n -> p k n", p=128)
                op("pool", "dma_start", dict(out=Wg[g][:, :, :], in_=src), w=[Wg[g]], dma=True, semkey="wg%d" % g)
                if g == 0:
                    op("pool", "dma_start", dict(out=Wbd[:, :, :], in_=w_bd[l].rearrange("(k p) n -> p k n", p=128)), w=[Wbd], dma=True, semkey="wbd")
            else:
                op("pool", "dma_start", dict(out=Wo[:, :, :], in_=w_out[l].rearrange("(k p) n -> p k n", p=128)), w=[Wo], dma=True, semkey="wo")

        def load_weights(l):
            for g in range(4):
                n = Wg[g].h.shape[2]
                src = w_in[l, :, Wgoff[g]:Wgoff[g] + n].rearrange("(k p) n -> p k n", p=128)
                op("pool", "dma_start", dict(out=Wg[g][:, :, :], in_=src), w=[Wg[g]], dma=True,
                   semkey="wg%d" % g)
                if g == 0:
                    op("pool", "dma_start", dict(out=Wbd[:, :, :], in_=w_bd[l].rearrange("(k p) n -> p k n", p=128)),
                       w=[Wbd], dma=True, semkey="wbd")
            op("pool", "dma_start", dict(out=Wo[:, :, :], in_=w_out[l].rearrange("(k p) n -> p k n", p=128)),
               w=[Wo], dma=True, semkey="wo")

        load_weights(0)
        for l in range(L):
            o = l * P_PER
            op("act", "activation", dict(out=negA[:, l * 4:l * 4 + 4], in_=pc(o + P_ALOG, 4), func=AF.Exp),
               r=[prm], w=[negA])
            op("dve", "tensor_scalar", dict(out=hdng[:, l:l + 1], in0=pc(o + P_DNG), scalar1=0.5, scalar2=None,
                                                          op0=ALU.mult), r=[prm], w=[hdng])
        op("dve", "tensor_scalar", dict(out=negA[:, :], in0=negA[:, :], scalar1=-1.0, scalar2=None, op0=ALU.mult),
           r=[negA], w=[negA])
        for l in range(L):
            o = l * P_PER
            op("dve", "tensor_scalar", dict(out=hcc[:, l * 62:(l + 1) * 62], in0=pc(o + P_CC, 62), scalar1=0.5, scalar2=None, op0=ALU.mult),
               r=[prm], w=[hcc])
            op("dve", "tensor_scalar", dict(out=hcl[:, l * 4:l * 4 + 2], in0=pc(o + P_CLG, 2), scalar1=0.5, scalar2=None, op0=ALU.mult),
               r=[prm], w=[hcl])
            op("dve", "tensor_scalar", dict(out=hcl[:, l * 4 + 2:l * 4 + 4], in0=pc(o + P_CLB, 2), scalar1=0.5, scalar2=None, op0=ALU.mult),
               r=[prm], w=[hcl])

        s1done = set()

        def stage1_gen(l2, ti2):
            s1done.add((l2, ti2))
            sample2 = (ti2 == 4)
            PT2 = 64 if sample2 else 128
            nsub2 = 1 if sample2 else 4
            po2 = l2 * P_PER
            src2 = xin if l2 == 0 else y1
            ycf_ = yc[:, :, :].rearrange("p a b -> p (a b)")
            for s in range(nsub2):
                r0 = ti2 * 512 + s * 128
                x_ = ycf_
                yres = y1res[ti2 * 4 + s]
                op("act", "dma_start", dict(out=x_[:PT2, :], in_=src2[r0:r0 + PT2, :]),
                   r=([yres] if l2 > 0 else []), w=[yc], dma=True, semkey="xf")
                yield
                op("act", "activation", dict(out=xh[:PT2, :], in_=x_[:PT2, :], func=AF.Square, accum_out=ss[:PT2, s:s + 1]), r=[yc], w=[xh, ss])
                op("act", "activation", dict(out=rs[:PT2, s:s + 1], in_=ss[:PT2, s:s + 1], func=AF.Ln, scale=1.0 / D,
                                             bias=cf[:PT2, F_EPS:F_EPS + 1]), r=[ss, cf], w=[rs])
                op("act", "activation", dict(out=rs[:PT2, s:s + 1], in_=rs[:PT2, s:s + 1], func=AF.Exp, scale=-0.5), r=[rs], w=[rs])
                yield
                op("act", "activation", dict(out=xh[:PT2, :], in_=x_[:PT2, :], func=AF.Copy, scale=rs[:PT2, s:s + 1]), r=[yc, rs], w=[xh])
                yield
                yield
                yield
                b = banks[7]
                for kc in range(8):
                    op("pe", "transpose", dict(out=bfv(b)[:, kc * 128:kc * 128 + PT2], in_=xh[:PT2, kc * 128:(kc + 1) * 128],
                                               identity=identb[:PT2, :PT2]), r=[xh, cb], w=[b])
                op("dve", "tensor_tensor", dict(out=hT[:, :, s * 128:s * 128 + PT2], in0=bfv(b).rearrange("p (k t) -> p k t", k=8)[:, :, :PT2],
                                                in1=pc(po2 + P_NG, 8).unsqueeze(2).to_broadcast([128, 8, PT2]), op=ALU.mult), r=[b, prm], w=[hT])
                yield

        def layer(l):
            po = l * P_PER
            last_layer = (l == nlayers - 1)
            op("pool", "memset", dict(ap=hist[:, :, :], constant=0.0), w=[hist])
            op("pool", "memset", dict(ap=hsb[:, :, :], constant=0.0), w=[hsb])
            op("pool", "memset", dict(ap=ub[:, :, :], constant=0.0), w=[ub])
            op("pool", "memset", dict(ap=S[:, :], constant=0.0), w=[S])
            op("pool", "memset", dict(ap=Sbf[:, :], constant=0.0), w=[Sbf])

            src_x = xin if l == 0 else y1

            for ti in range(ntiles):
                sample = (ti == 4)
                tok0 = ti * 512
                TW = 64 if sample else 512
                nsub = 1 if sample else 4
                PT = 64 if sample else 128
                nseq = NSEQ_S if sample else 1
                TT = TS if sample else 512
                lastp = (ti == 3)

                def v3(ap2):
                    return ap2.rearrange("p (q t) -> p q t", q=nseq)

                if (l, ti) not in s1done:
                    for _ in stage1_gen(l, ti):
                        pass

                def proj(col):
                    g = max(i for i in range(4) if Wgoff[i] <= col)
                    lc = col - Wgoff[g]
                    b = bank()
                    for kc in range(8):
                        op("pe", "matmul", dict(out=b[:, :TW], lhsT=Wg[g][:, kc, lc:lc + 128],
                                                                              rhs=hT[:, kc, :TW], start=(kc == 0), stop=(kc == 7)),
                           r=[Wg[g], hT], w=[b])
                    return b

                def proj_y(col):
                    g = max(i for i in range(4) if Wgoff[i] <= col)
                    lc = col - Wgoff[g]
                    b = bank()
                    for kc in range(8):
                        op("pe", "matmul", dict(out=b[:, :TW], lhsT=Wg[g][:, kc, lc:lc + 128], rhs=hT[:, kc, :TW], start=(kc == 0), stop=(kc == 7)),
                           r=[Wg[g], hT], w=[b])
                        if kc % 2 == 1 and kc < 7:
                            yield
                    return b

                def silu2(b, out_ap, outT, eng="dve"):
                    thc[0] += 1
                    th = bt["th"] if thc[0] % 2 else bt["th2"]
                    op("act", "activation", dict(out=th[:, :TW], in_=b[:, :TW], func=AF.Tanh, scale=0.5), r=[b], w=[th])
                    op(eng, "scalar_tensor_tensor", dict(out=out_ap, in0=th[:, :TW], scalar=1.0, in1=b[:, :TW],
                                                             op0=ALU.add, op1=ALU.mult), r=[th, b], w=[outT])

                if sample:
                    op("sp", "dma_start", dict(out=stg[:48, :1024], in_=st_qkv[l][:, 0:1024]), w=[stg], dma=True, semkey="stg")
                    op("sp", "dma_start", dict(out=stg2[:48, :512], in_=st_qkv[l][:, 1024:1536]), w=[stg2], dma=True, semkey="stg2")
                    for c in range(12):
                        b = bank()
                        sg_ = stg if c < 8 else stg2
                        op("pe", "transpose", dict(out=b[:, :48], in_=sg_[:48, (c % 8) * 128:(c % 8 + 1) * 128],
                                                                   identity=identf[:48, :48]), r=[sg_, cf], w=[b])
                        op("dve", "tensor_copy", dict(out=raw_s[:, c, :, 0:3],
                                                                    in_=b[:, :48].rearrange("p (q r) -> p q r", q=NSEQ_S)),
                           r=[b], w=[raw_s])
                    op("sp", "dma_start", dict(out=stg2[:32, :256], in_=st_sc[l]), w=[stg2], dma=True, semkey="stg2")
                    for j in range(2):
                        b = bank()
                        op("pe", "transpose", dict(out=b[:, :32], in_=stg2[:32, j * 128:(j + 1) * 128],
                                                                   identity=identf[:32, :32]), r=[stg2, cf], w=[b])
                        op("dve", "tensor_copy", dict(
                            out=hsb[:, j, 0:96].rearrange("p (q t) -> p q t", q=NSEQ_S)[:, :, 0:2],
                            in_=b[:, :32].rearrange("p (q r) -> p q r", q=NSEQ_S)), r=[b], w=[hsb])
                    op("sp", "dma_start", dict(out=stg[:120, :1024].rearrange("p (q c) -> p q c", q=4),
                                                   in_=st_cc[l].rearrange("(q p) c -> p q c", p=120)),
                       w=[stg], dma=True, semkey="stg")
                    for q in range(4):
                        for j in range(2):
                            b = bank()
                            op("pe", "transpose", dict(
                                out=b[:, :120], in_=stg[:120, q * 256 + j * 128:q * 256 + (j + 1) * 128],
                                identity=identf[:120, :120]), r=[stg, cf], w=[b])
                            op("act", "activation", dict(
                                out=ub[:, j, :].rearrange("p (q t) -> p q t", q=NSEQ_S)[:, 4 * q:4 * q + 4, 0:30],
                                in_=b[:, :120].rearrange("p (q r) -> p q r", q=4), func=AF.Copy, scale=2.0), r=[b], w=[ub])

                bbd = bank()
                for s in range(nsub):
                    for kc in range(8):
                        op("pe", "matmul", dict(out=bbd[:PT, s * 8:(s + 1) * 8], lhsT=hT[:, kc, s * 128:s * 128 + PT],
                                                                  rhs=Wbd[:, kc, :], start=(kc == 0), stop=(kc == 7)),
                           r=[hT, Wbd], w=[bbd])
                op("dve", "tensor_copy", dict(out=bdS[:PT, :nsub, :], in_=bbd[:PT, :nsub * 8].rearrange("p (s c) -> p s c", c=8)),
                   r=[bbd], w=[bdS])

                NS = slice(0, nsub)
                def scal_part1():
                    NS = slice(0, nsub)
                    op("dve", "tensor_tensor", dict(out=zz[:PT, NS, 0:4], in0=bdS[:PT, NS, 4:8],
                                                        in1=pc(po + P_DTB, 4)[:PT].unsqueeze(1).to_broadcast([PT, nsub, 4]), op=ALU.add),
                       r=[bdS, prm], w=[zz])
                    op("dve", "tensor_scalar", dict(out=zz[:PT, NS, 4:8], in0=bdS[:PT, NS, 0:4], scalar1=-1.0, scalar2=None,
                                                        op0=ALU.mult), r=[bdS], w=[zz])
                    op("dve", "scalar_tensor_tensor", dict(out=az[:PT, NS, :], in0=zz[:PT, NS, :], scalar=-1.0, in1=zz[:PT, NS, :],
                                                               op0=ALU.mult, op1=ALU.max), r=[zz], w=[az])
                    op("act", "activation", dict(out=lnin[:PT, NS, 0:8], in_=az[:PT, NS, :], func=AF.Exp, scale=-1.0), r=[az], w=[lnin])
                    op("dve", "tensor_scalar", dict(out=lnin[:PT, NS, 0:8], in0=lnin[:PT, NS, 0:8], scalar1=1.0, scalar2=None,
                                                        op0=ALU.add), r=[lnin], w=[lnin])
                    op("act", "activation", dict(out=lnout[:PT, NS, 0:8], in_=lnin[:PT, NS, 0:8], func=AF.Ln), r=[lnin], w=[lnout])
                    op("dve", "scalar_tensor_tensor", dict(out=spz[:PT, NS, :], in0=zz[:PT, NS, :], scalar=0.0, in1=lnout[:PT, NS, 0:8],
                                                               op0=ALU.max, op1=ALU.add), r=[zz, lnout], w=[spz])
                    op("dve", "tensor_tensor", dict(out=gg[:PT, NS, :], in0=spz[:PT, NS, 0:4],
                                                        in1=negA[:PT, l * 4:l * 4 + 4].unsqueeze(1).to_broadcast([PT, nsub, 4]), op=ALU.mult),
                       r=[spz, negA], w=[gg])
                    tri = cf[:PT, (F_TRIS if sample else F_TRIP):(F_TRIS if sample else F_TRIP) + PT]
                    lmm = cf[:PT, (F_LMS if sample else F_LMP):(F_LMS if sample else F_LMP) + PT]
                    bg = bank()
                    for s in range(nsub):
                        op("pe", "matmul", dict(out=bg[:PT, s * 4:(s + 1) * 4], lhsT=tri, rhs=gg[:PT, s, :], start=True, stop=True),
                           r=[cf, gg], w=[bg])
                    op("dve", "tensor_copy", dict(out=gc[:PT, NS, :], in_=bg[:PT, :nsub * 4].rearrange("p (s c) -> p s c", c=4)),
                       r=[bg], w=[gc])
                    bl = bank()
                    for s in range(nsub):
                        op("pe", "matmul", dict(out=bl[:PT, s * 4:(s + 1) * 4], lhsT=lmm, rhs=gc[:PT, s, :], start=True, stop=True),
                           r=[cf, gc], w=[bl])
                    op("dve", "tensor_copy", dict(out=glt[:PT, NS, :], in_=bl[:PT, :nsub * 4].rearrange("p (s c) -> p s c", c=4)),
                       r=[bl], w=[glt])
                    op("act", "activation", dict(out=evb[:PT, NS, :], in_=spz[:PT, NS, 4:8], func=AF.Exp, scale=-1.0,
                                                     bias=cf[:PT, F_LN05:F_LN05 + 1]), r=[spz, cf], w=[evb])
                    if sample:
                        ng_ = 64
                        op("dve", "tensor_tensor", dict(
                            out=gsel[:PT, :].rearrange("p (q h) -> p q h", h=4),
                            in0=gc[:PT, 0, :].unsqueeze(1).to_broadcast([PT, NSEQ_S, 4]),
                            in1=cf[:PT, F_LASTS:F_LASTS + NSEQ_S].unsqueeze(2).to_broadcast([PT, NSEQ_S, 4]), op=ALU.mult),
                           r=[gc, cf], w=[gsel])
                    else:
                        ng_ = 16
                        op("dve", "tensor_scalar", dict(out=gsel[:, :16], in0=gc[:, :, :].rearrange("p s h -> p (s h)"),
                                                            scalar1=cf[:, F_LASTP:F_LASTP + 1], scalar2=None, op0=ALU.mult),
                           r=[gc, cf], w=[gsel])
                    bd_ = bank()
                    op("pe", "matmul", dict(out=bd_[:, :ng_], lhsT=cf[:PT, F_ONES:F_ONES + 128], rhs=gsel[:PT, :ng_], start=True, stop=True),
                       r=[cf, gsel], w=[bd_])
                    op("act", "activation", dict(out=dec[:, :ng_], in_=bd_[:, :ng_], func=AF.Exp), r=[bd_], w=[dec])


                def sumsq_block():
                    bss = bank()
                    sqb = [sqs, sqs2]

                    def sq_(c2):
                        op("pool", "tensor_tensor", dict(out=sqb[c2 % 2][:, :TW], in0=qkvT[:, c2, :TW], in1=qkvT[:, c2, :TW], op=ALU.mult),
                           r=[qkvT], w=[sqb[c2 % 2]])
                    sq_(0)
                    for c in range(8):
                        if c + 1 < 8:
                            sq_(c + 1)
                        if c % 2 == 0:
                            bgd_ = proj(C_GD + (c // 2) * 128)
                            silu2(bgd_, gd2[:, c // 2, :TW], gd2)
                        for s in range(nsub):
                            op("pe", "matmul", dict(out=bss[:PT, s * 8 + c:s * 8 + c + 1], lhsT=sqb[c % 2][:, s * 128:s * 128 + PT],
                                                    rhs=cb[:, B_ONES:B_ONES + 1], start=True, stop=True), r=[sqb[c % 2], cb], w=[bss])
                    op("dve", "tensor_scalar", dict(out=lnin[:PT, NS, 8:16], in0=bss[:PT, :nsub * 8].rearrange("p (s c) -> p s c", c=8),
                                                    scalar1=EPS, scalar2=None, op0=ALU.add), r=[bss], w=[lnin])

                def scal_part2():
                    op("act", "activation", dict(out=lnout[:PT, NS, 8:16], in_=lnin[:PT, NS, 8:16], func=AF.Ln), r=[lnin], w=[lnout])
                    op("dve", "tensor_scalar", dict(out=hk[:PT, NS, :], in0=lnout[:PT, NS, 12:16], scalar1=0.5, scalar2=None, op0=ALU.mult),
                       r=[lnout], w=[hk])
                    op("dve", "scalar_tensor_tensor", dict(out=R3[:PT, NS, :, 1], in0=gc[:PT, NS, :], scalar=-1.0, in1=hk[:PT, NS, :],
                                                               op0=ALU.mult, op1=ALU.subtract), r=[gc, hk], w=[R3])
                    op("dve", "tensor_tensor", dict(out=tm1[:PT, NS, :], in0=gc[:PT, NS, :], in1=spz[:PT, NS, 4:8], op=ALU.subtract),
                       r=[gc, spz], w=[tm1])
                    op("dve", "tensor_tensor", dict(out=R3[:PT, NS, :, 0], in0=tm1[:PT, NS, :], in1=hk[:PT, NS, :], op=ALU.subtract),
                       r=[tm1, hk], w=[R3])
                    op("dve", "tensor_scalar", dict(out=tm2[:PT, NS, :], in0=lnout[:PT, NS, 8:12], scalar1=-0.5,
                                                        scalar2=-math.log(math.sqrt(128.0)), op0=ALU.mult, op1=ALU.add), r=[lnout], w=[tm2])
                    op("dve", "tensor_tensor", dict(out=R3[:PT, NS, :, 2], in0=tm2[:PT, NS, :], in1=gc[:PT, NS, :], op=ALU.add),
                       r=[tm2, gc], w=[R3])
                    op("dve", "tensor_copy", dict(out=R9[:PT, NS, :, 0:3], in_=R3[:PT, NS, :, :]), r=[R3], w=[R9])
                    op("dve", "tensor_tensor", dict(out=R3r[:PT, NS, :, :], in0=R3[:PT, NS, :, :], in1=R9[:PT, NS, :, 0:3], op=ALU.subtract), r=[R3, R9], w=[R3r])
                    op("dve", "tensor_copy", dict(out=R9[:PT, NS, :, 3:6], in_=R3r[:PT, NS, :, :]), r=[R3r], w=[R9])
                    op("dve", "tensor_tensor", dict(out=R3r[:PT, NS, :, :], in0=R3r[:PT, NS, :, :], in1=R9[:PT, NS, :, 3:6], op=ALU.subtract), r=[R3r, R9], w=[R3r])
                    op("dve", "tensor_copy", dict(out=R9[:PT, NS, :, 6:9], in_=R3r[:PT, NS, :, :]), r=[R3r], w=[R9])
                    op("act", "activation", dict(out=ea[:PT, NS, :], in_=R3[:PT, NS, :, 0], func=AF.Exp), r=[R3], w=[ea])
                    op("act", "activation", dict(out=ec[:PT, NS, :], in_=R3[:PT, NS, :, 2], func=AF.Exp), r=[R3], w=[ec])
                    op("dve", "tensor_tensor", dict(out=tm1[:PT, NS, :], in0=glt[:PT, NS, :], in1=R3[:PT, NS, :, 1], op=ALU.add),
                       r=[glt, R3], w=[tm1])
                    op("act", "activation", dict(out=es[:PT, NS, :], in_=tm1[:PT, NS, :], func=AF.Exp), r=[tm1], w=[es])

                scal_part1()
                P.fill = NFILL

                def qkv_iter(c):
                        b = proj(C_QKV + c * 128)
                        if sample:
                            src_full = raw_s
                            op("act", "activation", dict(out=raw_s[:, c, :, 3:7], in_=v3(b[:, :TW]), func=AF.Copy),
                               r=[b], w=[raw_s])
                            op("dve", "tensor_copy", dict(out=rawf_s[:, :, :].rearrange("p a b -> p (a b)")[:, c * 64:(c + 1) * 64], in_=b[:, :TW]), r=[b], w=[rawf_s])
                            rhs = [raw_s[:, c, :, k:k + TS] for k in range(4)]
                            rres = raw_s
                        else:
                            r_ = rr[c % 3]
                            op("act", "activation", dict(out=r_[:, 3:515], in_=b[:, :TW], func=AF.Copy), r=[b], w=[r_])
                            op("dve", "tensor_copy", dict(out=r_[:, 0:3], in_=hist[:, c, 0:3]), r=[hist], w=[r_])
                            if lastp:
                                op("dve", "tensor_copy", dict(out=rawlast[:, c, 0:3], in_=b[:, 509:512]), r=[b], w=[rawlast])
                            rhs = [r_[:, k:k + 512] for k in range(4)]
                            rres = r_
                        qslots = []
                        for k in range(4):
                            dslot = dcr[dcc[0] % NSLOT]
                            dcc[0] += 1
                            qslots.append(dslot)
                            if ti > 0 and not sample:
                                op("pool", "tensor_scalar", dict(out=dslot[:, :], in0=identb, scalar1=pc(po + P_CQ + c * 4 + k), scalar2=1.0,
                                                                 op0=ALU.mult, op1=ALU.mult), r=[cb, prm], w=[dslot])
                            elif dcc[0] % 2:
                                op("act", "activation", dict(out=dslot[:, :], in_=identb, func=AF.Copy, scale=pc(po + P_CQ + c * 4 + k)),
                                   r=[cb, prm], w=[dslot])
                            else:
                                op("dve", "tensor_scalar", dict(out=dslot[:, :], in0=identb, scalar1=pc(po + P_CQ + c * 4 + k), scalar2=None,
                                                                op0=ALU.mult), r=[cb, prm], w=[dslot])
                        yield
                        b2 = bank()
                        for k in range(4):
                            dslot = qslots[k]
                            op("pe", "matmul", dict(out=(v3(b2[:, :TW]) if sample else b2[:, :TW]), lhsT=dslot[:, :], rhs=rhs[k],
                                                    start=(k == 0), stop=(k == 3)), r=[dslot, rres], w=[b2])
                        if not sample:
                            op("dve", "tensor_copy", dict(out=hist[:, c, 0:3], in_=r_[:, 512:515]), r=[r_], w=[hist])
                        silu2(b2, qkvT[:, c, :TW], qkvT)

                pend_ = None
                for c in range(12):
                    g_it = qkv_iter(c)
                    next(g_it)
                    if pend_ is not None:
                        for _ in pend_:
                            pass
                    pend_ = g_it
                for _ in pend_:
                    pass
                if sample and l + 1 < nlayers:
                    load_wgroup(l + 1, 0)

                if lastp or sample:
                    nr = 48 if sample else 3
                    bks = [bank(), bank(), bank()]
                    for c in range(12):
                        bb_ = bks[c // 4]
                        if sample:
                            op("pool", "tensor_copy", dict(out=csb[:, 0:48].rearrange("p (q t) -> p q t", q=NSEQ_S),
                                                           in_=rawf_s[:, :, :].rearrange("p a b -> p (a b)")[:, c * 64:(c + 1) * 64].rearrange("p (q t) -> p q t", q=NSEQ_S)[:, :, 1:4]),
                               r=[rawf_s], w=[csb])
                            src_, sres = csb[:, 0:48], csb
                        else:
                            src_, sres = rawlast[:, c, 0:3], rawlast
                        op("pe", "transpose", dict(out=bb_[:nr, (c % 4) * 128:(c % 4 + 1) * 128], in_=src_, identity=identf),
                           r=[sres, cf], w=[bb_])
                    op("dve", "tensor_copy", dict(out=stg[:nr, 0:512], in_=bks[0][:nr, :]), r=[bks[0]], w=[stg])
                    op("dve", "tensor_copy", dict(out=stg[:nr, 512:1024], in_=bks[1][:nr, :]), r=[bks[1]], w=[stg])
                    op("dve", "tensor_copy", dict(out=stg2[:nr, 0:512], in_=bks[2][:nr, :]), r=[bks[2]], w=[stg2])
                    dst = s_qkv[l] if sample else p_qkv[l]
                    op("sp", "dma_start", dict(out=dst[:, 0:1024], in_=stg[:nr, :1024]), r=[stg], dma=True, semkey="stg")
                    op("sp", "dma_start", dict(out=dst[:, 1024:1536], in_=stg2[:nr, :512]), r=[stg2], dma=True, semkey="stg2")

                sumsq_block()
                P.fill = 0
                scal_part2()

                if sample and l + 1 < nlayers:
                    load_wgroup(l + 1, 1)

                def branch_gen():
                    if sample:
                        def hs_v(j, k0, k1):
                            return hsb[:, j, 0:96].rearrange("p (q t) -> p q t", q=NSEQ_S)[:, :, k0:k1]
                    else:
                        def hs_v(j, k0, k1):
                            return hsb[:, j, k0:k1].unsqueeze(1)
                    for j in range(2):
                        bC = yield from proj_y(C_SC + j * 128)
                        op("act", "activation", dict(out=csb[:, :TW], in_=bC[:, :TW], func=AF.Copy), r=[bC], w=[csb])
                        yield
                        bX = yield from proj_y(C_SX + j * 128)
                        op("dve", "tensor_tensor", dict(out=hs_v(j, 2, 2 + TT), in0=v3(bX[:, :TW]), in1=v3(csb[:, :TW]), op=ALU.mult),
                           r=[bX, csb], w=[hsb])
                        yield
                        for k in range(3):
                            wk = pc(po + P_CS + j * 3 + k)
                            if k == 0:
                                op("dve", "tensor_scalar", dict(out=v3(ys[:, :TW]), in0=hs_v(j, 0, TT), scalar1=wk, scalar2=None, op0=ALU.mult),
                                   r=[hsb, prm], w=[ys])
                                yield
                            else:
                                op("dve", "scalar_tensor_tensor", dict(out=v3(ys[:, :TW]), in0=hs_v(j, k, k + TT), scalar=wk,
                                                                                            in1=v3(ys[:, :TW]), op0=ALU.mult, op1=ALU.add),
                                   r=[hsb, prm, ys], w=[ys])
                                yield
                        bGs = yield from proj_y(C_GS + j * 128)
                        silu2(bGs, g2[:, :TW], g2)
                        yield
                        op("pool", "tensor_tensor", dict(out=g2[:, :TW], in0=g2[:, :TW], in1=ys[:, :TW], op=ALU.mult), r=[g2, ys], w=[g2])
                        yield
                        bB_ = yield from proj_y(C_SB + j * 128)
                        op("dve", "scalar_tensor_tensor", dict(out=oT[:, 4 + j, :TW], in0=bB_[:, :TW], scalar=0.5, in1=g2[:, :TW],
                                                                                 op0=ALU.mult, op1=ALU.mult), r=[bB_, g2], w=[oT])
                        yield
                        if not sample:
                            if lastp:
                                pass
                            else:
                                op("pool", "tensor_copy", dict(out=hsb[:, j, 0:2], in_=hsb[:, j, 512:514]), r=[hsb], w=[hsb])
                                yield
                    if sample and l + 1 < nlayers:
                        load_wgroup(l + 1, 2)
                    if lastp or sample:
                        nr = 32 if sample else 2
                        b = bank()
                        for j in range(2):
                            if sample:
                                op("pool", "tensor_copy", dict(out=csb[:, j * 32:(j + 1) * 32].rearrange("p (q t) -> p q t", q=NSEQ_S),
                                                                        in_=hs_v(j, 4, 6)), r=[hsb], w=[csb])
                                yield
                                src_ = csb[:, j * 32:(j + 1) * 32]
                                sres = csb
                            else:
                                src_ = hsb[:, j, 512:514]
                                sres = hsb
                            op("pe", "transpose", dict(out=b[:nr, j * 128:(j + 1) * 128], in_=src_, identity=identf),
                               r=[sres, cf], w=[b])
                            yield
                        op("dve", "tensor_copy", dict(out=stg2[:nr, :256], in_=b[:nr, :256]), r=[b], w=[stg2])
                        yield
                        dst = s_sconv[l] if sample else p_sconv[l]
                        op("sp", "dma_start", dict(out=dst, in_=stg2[:nr, :256]), r=[stg2], dma=True, semkey="stg2")
                        yield

                    if sample:
                        def u_v(t_, j, k0, k1):
                            return t_[:, j, :].rearrange("p (q t) -> p q t", q=NSEQ_S)[:, :, k0:k1]
                    else:
                        def u_v(t_, j, k0, k1):
                            return t_[:, j, k0:k1].unsqueeze(1)
                    for j in range(2):
                        bGb = yield from proj_y(C_GB + j * 128)
                        thc[0] += 1
                        th = bt["th"] if thc[0] % 2 else bt["th2"]
                        op("act", "activation", dict(out=th[:, :TW], in_=bGb[:, :TW], func=AF.Tanh, scale=0.5), r=[bGb], w=[th])
                        yield
                        bGa = yield from proj_y(C_GA + j * 128)
                        op("dve", "scalar_tensor_tensor", dict(out=u_v(ub, j, 30, 30 + TT), in0=v3(th[:, :TW]), scalar=1.0,
                                                                                 in1=v3(bGa[:, :TW]), op0=ALU.add, op1=ALU.mult), r=[th, bGa], w=[ub])
                        yield
                        op("act", "activation", dict(out=ubf[:, j, :], in_=ub[:, j, :], func=AF.Copy), r=[ub], w=[ubf])
                        yield
                        b2 = bank()
                        def build_cc(k):
                            dslot = dcr[dcc[0] % NSLOT]
                            dcc[0] += 1
                            if ti > 0 and not sample:
                                op("pool", "tensor_scalar", dict(out=dslot[:, :], in0=identb, scalar1=pc(po + P_CC + j * 31 + k), scalar2=0.5,
                                                                 op0=ALU.mult, op1=ALU.mult), r=[cb, prm], w=[dslot])
                            elif dcc[0] % 2:
                                op("act", "activation", dict(out=dslot[:, :], in_=identb, func=AF.Copy, scale=hcc[:, l * 62 + j * 31 + k:l * 62 + j * 31 + k + 1]),
                                   r=[cb, hcc], w=[dslot])
                            else:
                                op("dve", "tensor_scalar", dict(out=dslot[:, :], in0=identb, scalar1=hcc[:, l * 62 + j * 31 + k:l * 62 + j * 31 + k + 1], scalar2=None,
                                                                op0=ALU.mult), r=[cb, hcc], w=[dslot])
                            return dslot
                        slots_ = {}
                        for k in range(CLEAD):
                            slots_[k] = build_cc(k)
                            yield
                        for k in range(31):
                            if k + CLEAD < 31:
                                slots_[k + CLEAD] = build_cc(k + CLEAD)
                                yield
                            dslot = slots_.pop(k)
                            op("pe", "matmul", dict(out=v3(b2[:, :TW]) if sample else b2[:, :TW], lhsT=dslot[:, :],
                                                    rhs=u_v(ubf, j, k, k + TT) if sample else ubf[:, j, k:k + TT],
                                                    start=(k == 0), stop=(k == 30)), r=[dslot, ubf], w=[b2])
                            yield
                        op("act", "activation", dict(out=yc[:, j, :TW], in_=b2[:, :TW], func=AF.Identity, bias=pc(po + P_CB + j)),
                           r=[b2, prm], w=[yc])
                        yield
                        op("pool", "tensor_tensor", dict(out=ysq[j][:, :TW], in0=yc[:, j, :TW], in1=yc[:, j, :TW], op=ALU.mult), r=[yc], w=[ysq[j]])
                        yield
                        if not sample and not lastp:
                            op("pool", "tensor_copy", dict(out=ub[:, j, 0:30], in_=ub[:, j, 512:542]), r=[ub], w=[ub])
                            yield
                    bLS = bank()
                    for s in range(nsub):
                        for j in range(2):
                            op("pe", "matmul", dict(out=bLS[:PT, s * 2:s * 2 + 1], lhsT=yc[:, j, s * 128:s * 128 + PT], rhs=cf[:, F_ONES:F_ONES + 1],
                                                    start=(j == 0), stop=(j == 1)), r=[yc, cf], w=[bLS])
                            yield
                        for j in range(2):
                            op("pe", "matmul", dict(out=bLS[:PT, s * 2 + 1:s * 2 + 2], lhsT=ysq[j][:, s * 128:s * 128 + PT], rhs=cb[:, B_ONES:B_ONES + 1],
                                                    start=(j == 0), stop=(j == 1)), r=[ysq[j], cb], w=[bLS])
                            yield
                    NS2 = slice(0, nsub)
                    op("dve", "tensor_scalar", dict(out=lst[:PT, NS2, :], in0=bLS[:PT, :nsub * 2].rearrange("p (s c) -> p s c", c=2), scalar1=1.0 / 256,
                                                    scalar2=None, op0=ALU.mult), r=[bLS], w=[lst])
                    yield
                    op("dve", "tensor_tensor", dict(out=lvr[:PT, NS2], in0=lst[:PT, NS2, 0], in1=lst[:PT, NS2, 0], op=ALU.mult), r=[lst], w=[lvr])
                    yield
                    op("dve", "tensor_tensor", dict(out=lvr[:PT, NS2], in0=lst[:PT, NS2, 1], in1=lvr[:PT, NS2], op=ALU.subtract), r=[lst, lvr], w=[lvr])
                    yield
                    op("pool", "tensor_scalar", dict(out=lvr[:PT, NS2], in0=lvr[:PT, NS2], scalar1=EPS, scalar2=None, op0=ALU.add), r=[lvr], w=[lvr])
                    yield
                    op("pool", "tensor_tensor", dict(out=LP[:PT, NS2, 0], in0=lvr[:PT, NS2], in1=cf[:PT, F_MHALF:F_MHALF + 1].to_broadcast([PT, nsub]),
                                                     op=ALU.pow), r=[lvr, cf], w=[LP])
                    yield
                    op("dve", "scalar_tensor_tensor", dict(out=LP[:PT, NS2, 1], in0=lst[:PT, NS2, 0], scalar=-1.0, in1=LP[:PT, NS2, 0],
                                                           op0=ALU.mult, op1=ALU.mult), r=[lst, LP], w=[LP])
                    yield
                    bLR = bank()
                    for s in range(nsub):
                        op("pe", "transpose", dict(out=bLR[:2, s * 128:s * 128 + PT], in_=LP[:PT, s, :], identity=identf[:PT, :PT]), r=[LP, cf], w=[bLR])
                        yield
                    op("dve", "tensor_copy", dict(out=LR[:2, :TW], in_=bLR[:2, :TW]), r=[bLR], w=[LR])
                    yield
                    bLA, bLB = bank(), bank()
                    op("pe", "matmul", dict(out=bLA[:, :TW], lhsT=cf[0:2, F_SEL:F_SEL + 128], rhs=LR[0:2, :TW], start=True, stop=True), r=[cf, LR], w=[bLA])
                    yield
                    op("pe", "matmul", dict(out=bLB[:, :TW], lhsT=cf[0:2, F_SEL + 128:F_SEL + 256], rhs=LR[0:2, :TW], start=True, stop=True), r=[cf, LR], w=[bLB])
                    yield
                    for j in range(2):
                        op("dve", "tensor_tensor", dict(out=yc[:, j, :TW], in0=yc[:, j, :TW], in1=bLA[:, :TW], op=ALU.mult), r=[yc, bLA], w=[yc])
                        yield
                        op("dve", "tensor_tensor", dict(out=yc[:, j, :TW], in0=yc[:, j, :TW], in1=bLB[:, :TW], op=ALU.add), r=[yc, bLB], w=[yc])
                        yield
                        thc[0] += 1
                        th = bt["th"] if thc[0] % 2 else bt["th2"]
                        op("act", "activation", dict(out=th[:, :TW], in_=yc[:, j, :TW], func=AF.Tanh, scale=hcl[:, l * 4 + j:l * 4 + j + 1],
                                                     bias=hcl[:, l * 4 + 2 + j:l * 4 + 3 + j]), r=[yc, hcl], w=[th])
                        yield
                        op("act", "activation", dict(out=yc[:, j, :TW], in_=yc[:, j, :TW], func=AF.Identity, scale=pc(po + P_CLG + j),
                                                     bias=pc(po + P_CLB + j)), r=[yc, prm], w=[yc])
                        yield
                        op("dve", "scalar_tensor_tensor", dict(out=yc[:, j, :TW], in0=th[:, :TW], scalar=1.0, in1=yc[:, j, :TW], op0=ALU.add, op1=ALU.mult),
                           r=[th, yc], w=[yc])
                        yield
                        bGc = yield from proj_y(C_GC + j * 128)
                        silu2(bGc, g2[:, :TW], g2)
                        yield
                        op("dve", "scalar_tensor_tensor", dict(out=oT[:, 6 + j, :TW], in0=yc[:, j, :TW], scalar=0.25, in1=g2[:, :TW],
                                                               op0=ALU.mult, op1=ALU.mult), r=[yc, g2], w=[oT])
                        yield
                    if sample and l + 1 < nlayers:
                        load_wgroup(l + 1, 3)
                    if lastp:
                        b = bank()
                        for j in range(2):
                            op("pe", "transpose", dict(out=b[:30, j * 128:(j + 1) * 128], in_=ub[:, j, 512:542], identity=identf),
                               r=[ub, cf], w=[b])
                            yield
                        op("act", "activation", dict(out=stg2[:30, :256], in_=b[:30, :256], func=AF.Copy, scale=0.5), r=[b], w=[stg2])
                        yield
                        op("sp", "dma_start", dict(out=p_cconv[l], in_=stg2[:30, :256]), r=[stg2], dma=True, semkey="stg2")
                        yield
                    if sample:
                        for q in range(4):
                            b = bank()
                            for j in range(2):
                                op("pool", "tensor_copy", dict(out=mu[:, j * 120:(j + 1) * 120].rearrange("p (q t) -> p q t", q=4),
                                                                             in_=u_v(ub, j, 4, 34)[:, 4 * q:4 * q + 4, :]), r=[ub], w=[mu])
                                yield
                                op("pe", "transpose", dict(out=b[:120, j * 128:(j + 1) * 128], in_=mu[:, j * 120:(j + 1) * 120],
                                                                           identity=identf), r=[mu, cf], w=[b])
                                yield
                            op("act", "activation", dict(out=stg[:120, q * 256:(q + 1) * 256], in_=b[:120, :256], func=AF.Copy, scale=0.5),
                               r=[b], w=[stg])
                            yield
                        op("sp", "dma_start", dict(out=s_cconv[l].rearrange("(q p) c -> p q c", p=120),
                                                       in_=stg[:120, :1024].rearrange("p (q c) -> p q c", q=4)), r=[stg], dma=True, semkey="stg")
                        yield

                negS = cb[:PT, (B_NSS if sample else B_NSP):(B_NSS if sample else B_NSP) + PT]
                negI = cb[:PT, (B_NIS if sample else B_NIP):(B_NIS if sample else B_NIP) + PT]
                HW_ = 4 * PT
                sel = [cf[0:3, F_SEL + r_ * 128:F_SEL + r_ * 128 + PT] for r_ in range(3)]
                NG = 1 if sample else NGROUPS

                def hd(t_):
                    return t_[:PT, :].rearrange("p (h j) -> p h j", h=4)

                peng = "dve" if sample else "pool"

                def chunk_gen(s, hs, part):
                    HGn = len(hs)
                    h0 = hs[0]
                    cs_ = slice(s * 128, s * 128 + PT)
                    cP = slice(h0 * PT, (h0 + HGn) * PT)
                    cD = slice(h0 * 128, (h0 + HGn) * 128)
                    qT = {h: qkvT[:, h, cs_] for h in hs}
                    kT = {h: qkvT[:, 4 + h, cs_] for h in hs}
                    vT = {h: qkvT[:, 8 + h, cs_] for h in hs}

                    def g_(X):
                        return X.g(hs)

                    def v3p(ap2):
                        return ap2.rearrange("p (h j) -> p h j", h=HGn)

                    def v3d(ap2):
                        return ap2.rearrange("p (h j) -> p h j", h=HGn)

                    def maskg(col):
                        return cb[:PT, col:col + PT].unsqueeze(1).to_broadcast([PT, HGn, PT])

                    def bcs(t_):
                        return t_[:PT, s, h0:h0 + HGn].unsqueeze(2).to_broadcast([PT, HGn, 128])

                    gcnt = [h0 // max(HGn, 1)]

                    def evac(out_ap, in_ap, rr_, ww_, scale=None):
                        gcnt[0] += 1
                        e_ = "act" if gcnt[0] % 2 else "dve"
                        if e_ == "act":
                            kw_ = dict(out=out_ap, in_=in_ap, func=AF.Copy)
                            if scale is not None:
                                kw_["scale"] = scale
                            op("act", "activation", kw_, r=rr_, w=ww_)
                        elif scale is None:
                            op("dve", "tensor_copy", dict(out=out_ap, in_=in_ap), r=rr_, w=ww_)
                        else:
                            op("dve", "tensor_scalar", dict(out=out_ap, in0=in_ap, scalar1=scale, scalar2=None, op0=ALU.mult), r=rr_, w=ww_)

                    def mm(lt, rt, acc=None):
                        b_ = bank()
                        for h in hs:
                            hh = h - h0
                            if acc is not None:
                                op("pe", "matmul", dict(out=b_[:PT, hh * PT:(hh + 1) * PT], lhsT=identb[:PT, :PT],
                                                        rhs=acc[:PT, h * PT:(h + 1) * PT], start=True, stop=False), r=[acc.t[h], cb], w=[b_])
                            op("pe", "matmul", dict(out=b_[:PT, hh * PT:(hh + 1) * PT], lhsT=lt[:PT, h * PT:(h + 1) * PT],
                                                    rhs=rt[:PT, h * PT:(h + 1) * PT], start=(acc is None), stop=True), r=[lt.t[h], rt.t[h]], w=[b_])
                        return b_

                    Am, Bm, Kw = bt["Am"], bt["Bm"], bt["Kw"]
                    Mt, Ks, Vb = MtB[s % 2], KsB[s % 2], VbB[s % 2]

                    def stages_ad():
                        bR = bank()
                        for h in hs:
                            hh = h - h0
                            op("pe", "transpose", dict(out=bfv(bR)[:9, hh * 128:hh * 128 + PT], in_=R9[:PT, s, h, :], identity=identb[:PT, :PT]),
                               r=[R9, cb], w=[bR])
                        evac(Rw[:9, cD].rearrange("p (h j) -> p h j", h=HGn)[:, :, :PT],
                             bfv(bR)[:9, :HGn * 128].rearrange("p (h j) -> p h j", h=HGn)[:, :, :PT], [bR], g_(Rw))
                        yield
                        bEA, bEM = bank(), bank()
                        for h in hs:
                            hh = h - h0
                            rwh = Rw[0:9, h * 128:h * 128 + PT]
                            o_ = bEA[:PT, hh * PT:(hh + 1) * PT]
                            op("pe", "matmul", dict(out=o_, lhsT=cb[0:9, B_SELNB:B_SELNB + PT], rhs=rwh, start=True, stop=False), r=[Rw.t[h], cb], w=[bEA])
                            op("pe", "matmul", dict(out=o_, lhsT=identb[:PT, :PT], rhs=negS, start=False, stop=True), r=[cb], w=[bEA])
                            o2_ = bEM[:PT, hh * PT:(hh + 1) * PT]
                            op("pe", "matmul", dict(out=o2_, lhsT=cb[0:9, B_SELC:B_SELC + PT], rhs=rwh, start=True, stop=False), r=[Rw.t[h], cb], w=[bEM])
                            op("pe", "matmul", dict(out=o2_, lhsT=identb[:PT, :PT], rhs=negI, start=False, stop=True), r=[cb], w=[bEM])
                        for h in hs:
                            hh = h - h0
                            op("act", "activation", dict(out=EA[:PT, h * PT:(h + 1) * PT], in_=bEA[:PT, hh * PT:(hh + 1) * PT], func=AF.Exp,
                                                         bias=R3[:PT, s, h, 0:1]), r=[bEA, R3], w=[EA.t[h]])
                            op("act", "activation", dict(out=EM[:PT, h * PT:(h + 1) * PT], in_=bEM[:PT, hh * PT:(hh + 1) * PT], func=AF.Exp,
                                                         bias=R3[:PT, s, h, 1:2]), r=[bEM, R3], w=[EM.t[h]])
                        yield
                        bG, bP = bank(), bank()
                        for h in hs:
                            hh = h - h0
                            op("pe", "matmul", dict(out=bG[:PT, hh * PT:(hh + 1) * PT], lhsT=kT[h], rhs=kT[h], start=True, stop=True), r=[qkvT], w=[bG])
                            op("pe", "matmul", dict(out=bP[:PT, hh * PT:(hh + 1) * PT], lhsT=kT[h], rhs=qT[h], start=True, stop=True), r=[qkvT], w=[bP])
                        op("dve", "tensor_tensor", dict(out=Am[:PT, cP], in0=bG[:PT, :HGn * PT], in1=EA[:PT, cP], op=ALU.mult), r=[bG, g_(EA)], w=g_(Am))
                        op("dve", "tensor_tensor", dict(out=Mt[:PT, cP], in0=bP[:PT, :HGn * PT], in1=EM[:PT, cP], op=ALU.mult), r=[bP, g_(EM)], w=g_(Mt))
                        yield
                        bK, bV, bB = bank(), bank(), bank()
                        for h in hs:
                            hh = h - h0
                            op("pe", "transpose", dict(out=bfv(bK)[:PT, hh * 128:(hh + 1) * 128], in_=kT[h], identity=identb), r=[qkvT, cb], w=[bK])
                            op("pe", "transpose", dict(out=bfv(bV)[:PT, hh * 128:(hh + 1) * 128], in_=vT[h], identity=identb), r=[qkvT, cb], w=[bV])
                            op("pe", "transpose", dict(out=bfv(bB)[:PT, hh * PT:(hh + 1) * PT], in_=Am[:PT, h * PT:(h + 1) * PT],
                                                       identity=identb[:PT, :PT]), r=[Am.t[h], cb], w=[bB])
                        kv = v3d(bfv(bK)[:PT, :HGn * 128])
                        vv = v3d(bfv(bV)[:PT, :HGn * 128])
                        if HGn == 1:
                            op("act", "activation", dict(out=Kw[:PT, cD], in_=bfv(bK)[:PT, :128], func=AF.Copy, scale=ea[:PT, s, h0:h0 + 1]), r=[bK, ea], w=g_(Kw))
                            op("dve", "tensor_scalar", dict(out=Ks[:PT, cD], in0=bfv(bK)[:PT, :128], scalar1=es[:PT, s, h0:h0 + 1], scalar2=None, op0=ALU.mult),
                               r=[bK, es], w=g_(Ks))
                            op("act", "activation", dict(out=Vb[:PT, cD], in_=bfv(bV)[:PT, :128], func=AF.Copy, scale=evb[:PT, s, h0:h0 + 1]), r=[bV, evb], w=g_(Vb))
                        else:
                            op("dve", "tensor_tensor", dict(out=v3d(Kw[:PT, cD]), in0=kv, in1=bcs(ea), op=ALU.mult), r=[bK, ea], w=g_(Kw))
                            op("dve", "tensor_tensor", dict(out=v3d(Ks[:PT, cD]), in0=kv, in1=bcs(es), op=ALU.mult), r=[bK, es], w=g_(Ks))
                            op("dve", "tensor_tensor", dict(out=v3d(Vb[:PT, cD]), in0=vv, in1=bcs(evb), op=ALU.mult), r=[bV, evb], w=g_(Vb))
                        evac(Bm[:PT, cP], bfv(bB)[:PT, :HGn * PT], [bB], g_(Bm))
                        yield

                    if part != "main":
                        yield from stages_ad()
                        if part == "pre":
                            return
                    Pc, Qc, Tc, Uc = bt["Pa"], bt["Qa"], bt["Ta"], bt["Ua"]
                    Pn, Qn, Tn, Un = bt["Pb"], bt["Qb"], bt["Tb"], bt["Ub"]
                    op("dve", "tensor_tensor", dict(out=v3p(Pc[:PT, cP]), in0=v3p(Am[:PT, cP]), in1=maskg(B_D16), op=ALU.mult), r=[g_(Am), cb], w=g_(Pc))
                    op(peng, "tensor_tensor", dict(out=v3p(Qc[:PT, cP]), in0=v3p(Bm[:PT, cP]), in1=maskg(B_D16), op=ALU.mult), r=[g_(Bm), cb], w=g_(Qc))
                    op("dve", "tensor_tensor", dict(out=v3p(Tc[:PT, cP]), in0=maskg(B_ID), in1=v3p(Pc[:PT, cP]), op=ALU.subtract), r=[g_(Pc), cb], w=g_(Tc))
                    op(peng, "tensor_tensor", dict(out=v3p(Uc[:PT, cP]), in0=maskg(B_ID), in1=v3p(Qc[:PT, cP]), op=ALU.subtract), r=[g_(Qc), cb], w=g_(Uc))
                    yield
                    for k in range(1, 4):
                        b1 = mm(Qc, Pc)
                        evac(Pn[:PT, cP], b1[:PT, :HGn * PT], [b1], g_(Pn))
                        if k < 3:
                            b2 = mm(Pc, Qc)
                            evac(Qn[:PT, cP], b2[:PT, :HGn * PT], [b2], g_(Qn))
                        yield
                        b3 = mm(Uc, Pn)
                        op("dve", "tensor_tensor", dict(out=Tn[:PT, cP], in0=b3[:PT, :HGn * PT], in1=Tc[:PT, cP], op=ALU.add), r=[b3, g_(Tc)], w=g_(Tn))
                        b4 = mm(Pn, Uc)
                        op("dve", "tensor_tensor", dict(out=Un[:PT, cP], in0=b4[:PT, :HGn * PT], in1=Uc[:PT, cP], op=ALU.add), r=[b4, g_(Uc)], w=g_(Un))
                        Pc, Pn = Pn, Pc
                        Qc, Qn = Qn, Qc
                        Tc, Tn = Tn, Tc
                        Uc, Un = Un, Uc
                        yield
                    levels = [B_L32, B_L64] + ([] if sample else [B_L128])
                    Ao, Xm = bt["Ao"], bt["Xm"]
                    for li, mcol in enumerate(levels):
                        lastlev = (li == len(levels) - 1)
                        if not lastlev:
                            op(peng, "tensor_tensor", dict(out=v3p(Ao[:PT, cP]), in0=v3p(Bm[:PT, cP]), in1=maskg(mcol), op=ALU.mult), r=[g_(Bm), cb], w=g_(Ao))
                            b1 = mm(Ao, Tc)
                            evac(Xm[:PT, cP], b1[:PT, :HGn * PT], [b1], g_(Xm), scale=-1.0)
                            yield
                            b2 = mm(Uc, Xm)
                            op("dve", "tensor_tensor", dict(out=Tn[:PT, cP], in0=b2[:PT, :HGn * PT], in1=Tc[:PT, cP], op=ALU.add), r=[b2, g_(Tc)], w=g_(Tn))
                            yield
                            b3 = bank()
                            for h in hs:
                                hh = h - h0
                                op("pe", "transpose", dict(out=bfv(b3)[:PT, hh * PT:(hh + 1) * PT], in_=Tn[:PT, h * PT:(h + 1) * PT],
                                                           identity=identb[:PT, :PT]), r=[Tn.t[h], cb], w=[b3])
                            evac(Un[:PT, cP], bfv(b3)[:PT, :HGn * PT], [b3], g_(Un))
                            Tc, Tn = Tn, Tc
                            Uc, Un = Un, Uc
                            yield
                        else:
                            op(peng, "tensor_tensor", dict(out=v3p(Ao[:PT, cP]), in0=v3p(Am[:PT, cP]), in1=maskg(mcol), op=ALU.mult), r=[g_(Am), cb], w=g_(Ao))
                            b1 = mm(Ao, Uc)
                            evac(Xm[:PT, cP], b1[:PT, :HGn * PT], [b1], g_(Xm), scale=-1.0)
                            yield
                            b2 = mm(Tc, Xm)
                            op("dve", "tensor_tensor", dict(out=Un[:PT, cP], in0=b2[:PT, :HGn * PT], in1=Uc[:PT, cP], op=ALU.add), r=[b2, g_(Uc)], w=g_(Un))
                            Uc, Un = Un, Uc
                            yield
                    U = Uc
                    bW = bank()
                    for h in hs:
                        hh = h - h0
                        op("pe", "matmul", dict(out=bW[:, hh * PT:(hh + 1) * PT], lhsT=Kw[:PT, h * 128:(h + 1) * 128],
                                                rhs=U[:PT, h * PT:(h + 1) * PT], start=True, stop=True), r=[Kw.t[h], U.t[h]], w=[bW])
                    nwT = bt["nwT"]
                    evac(nwT[:, cP], bW[:, :HGn * PT], [bW], g_(nwT), scale=-1.0)
                    yield "chain"
                    vnew = bt["vnew"]
                    bO1 = bank()
                    if not sample:
                        bVn = bank()
                        for h in hs:
                            hh = h - h0
                            op("pe", "matmul", dict(out=bVn[:PT, hh * 128:(hh + 1) * 128], lhsT=U[:PT, h * PT:(h + 1) * PT],
                                                    rhs=Vb[:PT, h * 128:(h + 1) * 128], start=True, stop=False), r=[U.t[h], Vb.t[h]], w=[bVn])
                            op("pe", "matmul", dict(out=bVn[:PT, hh * 128:(hh + 1) * 128], lhsT=nwT[:, h * PT:(h + 1) * PT],
                                                    rhs=Sbf[:, h * 128:(h + 1) * 128], start=False, stop=True), r=[nwT.t[h], Sbf.t[h]], w=[bVn])
                        evac(vnew[:PT, cD], bVn[:PT, :HGn * 128], [bVn], g_(vnew))
                        for h in hs:
                            hh = h - h0
                            op("pe", "matmul", dict(out=bO1[:PT, hh * 128:(hh + 1) * 128], lhsT=qT[h], rhs=Sbf[:, h * 128:(h + 1) * 128],
                                                    start=True, stop=True), r=[qkvT, Sbf.t[h]], w=[bO1])
                    else:
                        bVT, bOT, bST = bank(), bank(), bank()
                        for h in hs:
                            op("pe", "matmul", dict(out=bVT[:, h * PT:(h + 1) * PT], lhsT=Vb[:PT, h * 128:(h + 1) * 128],
                                                    rhs=U[:PT, h * PT:(h + 1) * PT], start=True, stop=True), r=[Vb.t[h], U.t[h]], w=[bVT])
                        S0y = [EA, EM, csb, ys]

                        def ld1_(q):
                            s0_ = S0y[q % 4]
                            op("sp", "dma_start", dict(out=s0_[:, :].rearrange("p (h v) -> p h v", h=4),
                                                       in_=st_delta[l, q].rearrange("h k v -> k h v")), w=[s0_], dma=True, semkey="s0_%d" % (q % 4))
                        for q in range(4):
                            ld1_(q)
                        for q in range(NSEQ_S):
                            s0b = S0b[q % 3]
                            op("act", "activation", dict(out=s0b[:, :], in_=S0y[q % 4][:, :], func=AF.Copy), r=[S0y[q % 4]], w=[s0b])
                            if q + 4 < NSEQ_S:
                                ld1_(q + 4)
                            for h in hs:
                                op("pe", "matmul", dict(out=bST[:, h * PT + q * TS:h * PT + (q + 1) * TS], lhsT=s0b[:, h * 128:(h + 1) * 128],
                                                        rhs=nwT[:, h * PT + q * TS:h * PT + (q + 1) * TS], start=True, stop=True), r=[s0b, nwT], w=[bST])
                                op("pe", "matmul", dict(out=bOT[:, h * PT + q * TS:h * PT + (q + 1) * TS], lhsT=s0b[:, h * 128:(h + 1) * 128],
                                                        rhs=qkvT[:, h, q * TS:(q + 1) * TS], start=True, stop=True), r=[s0b, qkvT], w=[bOT])
                        vnT = bt["vnT"]
                        op("act", "activation", dict(out=t1[:, :HW_], in_=bVT[:, :HW_], func=AF.Copy), r=[bVT], w=[t1])
                        op("dve", "tensor_tensor", dict(out=vnT[:, :HW_], in0=t1[:, :HW_], in1=bST[:, :HW_], op=ALU.add), r=[t1, bST], w=[vnT])
                        op("dve", "tensor_copy", dict(out=o1T[:, :HW_], in_=bOT[:, :HW_]), r=[bOT], w=[o1T])
                        bVn = bank()
                        for h in hs:
                            op("pe", "transpose", dict(out=bfv(bVn)[:PT, h * 128:(h + 1) * 128], in_=vnT[:, h * PT:(h + 1) * PT], identity=identb),
                               r=[vnT, cb], w=[bVn])
                        op("act", "activation", dict(out=vnew[:PT, :], in_=bfv(bVn)[:PT, :512], func=AF.Copy), r=[bVn], w=[vnew])
                        for h in hs:
                            op("pe", "transpose", dict(out=bO1[:PT, h * 128:(h + 1) * 128], in_=o1T[:, h * PT:(h + 1) * PT], identity=identf),
                               r=[o1T, cf], w=[bO1])
                    op("dve", "tensor_tensor", dict(out=v3d(t1[:PT, cD]), in0=v3d(bO1[:PT, :HGn * 128]), in1=bcs(ec), op=ALU.mult), r=[bO1, ec], w=g_(t1))
                    yield
                    bO2 = bank()
                    for h in hs:
                        hh = h - h0
                        op("pe", "matmul", dict(out=bO2[:PT, hh * 128:(hh + 1) * 128], lhsT=Mt[:PT, h * PT:(h + 1) * PT],
                                                rhs=vnew[:PT, h * 128:(h + 1) * 128], start=True, stop=True), r=[Mt.t[h], vnew.t[h]], w=[bO2])
                    op("dve", "tensor_tensor", dict(out=ot[:PT, cD], in0=bO2[:PT, :HGn * 128], in1=t1[:PT, cD], op=ALU.add), r=[bO2, g_(t1)], w=g_(ot))
                    if not sample:
                        bS = bank()
                        for h in hs:
                            hh = h - h0
                            op("pe", "matmul", dict(out=bS[:, hh * 128:(hh + 1) * 128], lhsT=Ks[:PT, h * 128:(h + 1) * 128],
                                                    rhs=vnew[:PT, h * 128:(h + 1) * 128], start=True, stop=True), r=[Ks.t[h], vnew.t[h]], w=[bS])
                        for h in hs:
                            hh = h - h0
                            op("dve", "scalar_tensor_tensor", dict(out=S[:, h * 128:(h + 1) * 128], in0=S[:, h * 128:(h + 1) * 128],
                                                                   scalar=dec[:, s * 4 + h:s * 4 + h + 1], in1=bS[:, hh * 128:(hh + 1) * 128],
                                                                   op0=ALU.mult, op1=ALU.add), r=[S.t[h], dec, bS], w=[S.t[h]])
                        op("act", "activation", dict(out=Sbf[:, cD], in_=S[:, cD], func=AF.Copy), r=g_(S), w=g_(Sbf))
                    else:
                        Ksm = bt["Ksm"]
                        S0x = [EA, EM, csb, ys]
                        SNx = [t1, osq, g2]

                        def ld_(q):
                            s0_ = S0x[q % 4]
                            op("sp", "dma_start", dict(out=s0_[:, :].rearrange("p (h v) -> p h v", h=4),
                                                       in_=st_delta[l, q].rearrange("h k v -> k h v")), w=[s0_], dma=True, semkey="s0_%d" % (q % 4))
                        for q in range(4):
                            ld_(q)
                        for q in range(NSEQ_S):
                            s0, sn = S0x[q % 4], SNx[q % 3]
                            op("act", "activation", dict(out=Ksm[:PT, :], in_=Ks[:PT, :], func=AF.Copy, scale=cf[:PT, F_SEQM + q:F_SEQM + q + 1]),
                               r=[Ks, cf], w=[Ksm])
                            bS = bank()
                            for h in hs:
                                op("pe", "matmul", dict(out=bS[:, h * 128:(h + 1) * 128], lhsT=Ksm[:PT, h * 128:(h + 1) * 128],
                                                        rhs=vnew[:PT, h * 128:(h + 1) * 128], start=True, stop=True), r=[Ksm, vnew], w=[bS])
                            op("dve", "tensor_tensor", dict(out=sn[:, :].rearrange("p (h v) -> p h v", h=4), in0=s0[:, :].rearrange("p (h v) -> p h v", h=4),
                                                            in1=dec[:, q * 4:q * 4 + 4].unsqueeze(2).to_broadcast([128, 4, 128]), op=ALU.mult),
                               r=[s0, dec], w=[sn])
                            op("dve", "tensor_tensor", dict(out=sn[:, :], in0=sn[:, :], in1=bS[:, :], op=ALU.add), r=[sn, bS], w=[sn])
                            op("sp", "dma_start", dict(out=s_delta[l, q].rearrange("h k v -> k h v"), in_=sn[:, :].rearrange("p (h v) -> p h v", h=4)),
                               r=[sn], dma=True, semkey="sn_%d" % (q % 3))
                            if q + 4 < NSEQ_S:
                                ld_(q + 4)
                    yield
                    for h in hs:
                        op("act", "activation", dict(out=osq[:PT, h * 128:(h + 1) * 128], in_=ot[:PT, h * 128:(h + 1) * 128], func=AF.Square,
                                                     accum_out=ssqo[:PT, h:h + 1]), r=[ot.t[h]], w=[osq.t[h], ssqo.t[h]])
                    op("pool", "tensor_scalar", dict(out=rso[:PT, h0:h0 + HGn], in0=ssqo[:PT, h0:h0 + HGn], scalar1=1.0 / 128, scalar2=EPS,
                                                     op0=ALU.mult, op1=ALU.add), r=g_(ssqo), w=g_(rso))
                    op("pool", "tensor_tensor", dict(out=rso[:PT, h0:h0 + HGn], in0=rso[:PT, h0:h0 + HGn],
                                                     in1=cf[:PT, F_MHALF:F_MHALF + 1].to_broadcast([PT, HGn]), op=ALU.pow), r=[g_(rso), cf], w=g_(rso))
                    on = bt["on"]
                    yield
                    op("dve", "tensor_tensor", dict(out=v3d(on[:PT, cD]), in0=v3d(ot[:PT, cD]),
                                                    in1=rso[:PT, h0:h0 + HGn].unsqueeze(2).to_broadcast([PT, HGn, 128]), op=ALU.mult),
                       r=[g_(ot), g_(rso)], w=g_(on))
                    bOT2 = bank()
                    for h in hs:
                        hh = h - h0
                        op("pe", "transpose", dict(out=bfv(bOT2)[:, hh * PT:(hh + 1) * PT], in_=on[:PT, h * 128:(h + 1) * 128],
                                                   identity=identb[:PT, :PT]), r=[on.t[h], cb], w=[bOT2])
                    op("dve", "scalar_tensor_tensor", dict(out=oT[:, h0:h0 + HGn, cs_], in0=v3p(bfv(bOT2)[:, :HGn * PT]),
                                                           scalar=hdng[:, l:l + 1], in1=gd2[:, h0:h0 + HGn, cs_], op0=ALU.mult, op1=ALU.mult),
                       r=[bOT2, hdng, gd2], w=[oT.t[s]])

                xloaded = set()

                def xload_(s2):
                    if s2 in xloaded or s2 >= nsub:
                        return
                    xloaded.add(s2)
                    r2 = tok0 + s2 * 128
                    op("sp", "dma_start", dict(out=xb[s2 % 2][:PT, :], in_=src_x[r2:r2 + PT, :]),
                       r=([y1res[ti * 4 + s2]] if l > 0 else []), w=[xb[s2 % 2]], dma=True, semkey="x%d" % (s2 % 2))

                def outproj_gen(s):
                    r0 = tok0 + s * 128
                    x_ = xb[s % 2]
                    yres = y1res[ti * 4 + s]
                    xload_(s)
                    xload_(s + 1)
                    yield
                    for n in range(2):
                        b = bank()
                        for kc in range(8):
                            op("pe", "matmul", dict(out=b[:PT, :], lhsT=oT[:, kc, s * 128:s * 128 + PT], rhs=Wo[:, kc, n * 512:(n + 1) * 512],
                                                    start=(kc == 0), stop=(kc == 7)), r=[oT.t[s], Wo], w=[b])
                            if kc % 2 == 1:
                                yield
                        op("dve", "tensor_tensor", dict(out=x_[:PT, n * 512:(n + 1) * 512], in0=x_[:PT, n * 512:(n + 1) * 512],
                                                        in1=b[:PT, :], op=ALU.add), r=[x_, b], w=[x_])
                        yield
                    if not last_layer:
                        op("sp", "dma_start", dict(out=y1[r0:r0 + PT, :], in_=x_[:PT, :]), r=[x_], w=[yres], dma=True, semkey="yo%d" % (s % 2))
                    else:
                        ubff = ubf[:, :, :].rearrange("p a b -> p (a b)")
                        op("act", "activation", dict(out=ubff[:PT, 0:D], in_=x_[:PT, :], func=AF.Square, accum_out=ssf[:PT, 0:1]), r=[x_], w=[ubf, ssf])
                        op("act", "activation", dict(out=rsf[:PT, 0:1], in_=ssf[:PT, 0:1], func=AF.Ln, scale=1.0 / D,
                                                     bias=cf[:PT, F_EPS:F_EPS + 1]), r=[ssf, cf], w=[rsf])
                        op("act", "activation", dict(out=rsf[:PT, 0:1], in_=rsf[:PT, 0:1], func=AF.Exp, scale=-0.5), r=[rsf], w=[rsf])
                        yield
                        op("dve", "scalar_tensor_tensor", dict(out=x_[:PT, :], in0=x_[:PT, :], scalar=rsf[:PT, 0:1], in1=prm[:PT, P_FNG:P_FNG + D],
                                                               op0=ALU.mult, op1=ALU.mult), r=[x_, rsf, prm], w=[x_])
                        op("sp", "dma_start", dict(out=yout[r0:r0 + PT, :], in_=x_[:PT, :]), r=[x_], dma=True, semkey="yo%d" % (s % 2))
                    yield

                bg_ = branch_gen()
                e_queue = []
                e_gen = [None]
                f_started = [False]
                f_gen = [None]
                bg_live = [True]
                HG_ = 4 // NG
                grp = [list(range(g * HG_, (g + 1) * HG_)) for g in range(NG)]

                def side_streams(s):
                    if bg_live[0]:
                        bcur[0] = "C"
                        for _ in range(CSTEPS):
                            try:
                                next(bg_)
                            except StopIteration:
                                bg_live[0] = False
                                break
                    elif not sample:
                        bcur[0] = "C"
                        if not f_started[0]:
                            nxt = (l, ti + 1) if ti + 1 < ntiles else ((l + 1, 0) if l + 1 < nlayers else None)
                            f_started[0] = True
                            if nxt is not None:
                                f_gen[0] = stage1_gen(*nxt)
                        if f_gen[0] is not None:
                            try:
                                next(f_gen[0])
                            except StopIteration:
                                f_gen[0] = None
                        for _ in range(ESTEPS):
                            if e_gen[0] is None and e_queue:
                                e_gen[0] = outproj_gen(e_queue.pop(0))
                            if e_gen[0] is None:
                                break
                            try:
                                next(e_gen[0])
                            except StopIteration:
                                e_gen[0] = None

                def run_all(gl):
                    live_ = list(gl)
                    while live_:
                        for gen_ in list(live_):
                            try:
                                next(gen_)
                            except StopIteration:
                                live_.remove(gen_)

                bcur[0] = "D"
                run_all([chunk_gen(0, hs_, "pre") for hs_ in grp])
                for s in range(nsub):
                    mains = [chunk_gen(s, hs_, "main") for hs_ in grp]
                    chain_seen = [False] * len(mains)
                    pres = []
                    pre_started = False
                    live = list(range(len(mains)))
                    while live:
                        bcur[0] = "D"
                        if sample and any(chain_seen) and bg_live[0]:
                            bcur[0] = "C"
                            for _ in bg_:
                                pass
                            bg_live[0] = False
                            bcur[0] = "D"
                        for gi in list(live):
                            try:
                                if next(mains[gi]) == "chain":
                                    chain_seen[gi] = True
                            except StopIteration:
                                live.remove(gi)
                        if (not pre_started) and all(chain_seen) and s + 1 < nsub:
                            pres = [chunk_gen(s + 1, hs_, "pre") for hs_ in grp]
                            pre_started = True
                        for gen_ in list(pres):
                            try:
                                next(gen_)
                            except StopIteration:
                                pres.remove(gen_)
                        side_streams(s)
                    bcur[0] = "D"
                    if s + 1 < nsub and not pre_started:
                        pres = [chunk_gen(s + 1, hs_, "pre") for hs_ in grp]
                    run_all(pres)
                    e_queue.append(s)
                bcur[0] = "C"
                for _ in bg_:
                    pass
                if e_gen[0] is not None:
                    for _ in e_gen[0]:
                        pass
                if f_gen[0] is not None:
                    for _ in f_gen[0]:
                        pass
                bcur[0] = "all"
                if lastp:
                    op("sp", "dma_start", dict(out=p_delta[l].rearrange("h k v -> k h v"), in_=S[:, :].rearrange("p (h v) -> p h v", h=4)),
                       r=[S], dma=True, semkey="pdelta")


                for s_ in e_queue:
                    for _ in outproj_gen(s_):
                        pass
                if sample and l + 1 < nlayers:
                    load_wgroup(l + 1, 4)

        for l in range(nlayers):
            layer(l)
        nsem = P.emit()
        print('instr counts', {e: len(v) for e, v in P.lists.items()}, 'sems', nsem)
    return nc


_NC_CACHE = {}


def kernel(**inp):
    inp = {k: np.asarray(v) for k, v in inp.items()}
    cfc, cbc = make_consts()
    prm = pack_params(inp)
    w_in = inp["w_in"]
    w_in_r = np.ascontiguousarray(np.concatenate([w_in[:, :, 0:1536], w_in[:, :, 1544:3848]], axis=2))
    w_bd = np.ascontiguousarray(w_in[:, :, 1536:1544])
    w_out = np.ascontiguousarray(inp["w_out"])
    if "nc" not in _NC_CACHE:
        _NC_CACHE["nc"] = build_nc()
    nc = _NC_CACHE["nc"]
    in_maps = []
    for b in range(NCORES):
        sl = slice(b * NSEQ_S, (b + 1) * NSEQ_S)
        xin = np.concatenate([inp["x_prompt"][b], inp["x_sample"][sl].reshape(NSEQ_S * TS, D)], axis=0)
        in_maps.append({
            "xin": np.ascontiguousarray(xin, dtype=np.float32),
            "w_in": w_in_r, "w_bd": w_bd, "w_out": w_out, "prm": prm, "cf": cfc, "cb": cbc,
            "st_delta": np.ascontiguousarray(inp["state_delta"][:, sl]),
            "st_qkv": np.ascontiguousarray(inp["state_qkv_conv"][:, sl].reshape(L, NSEQ_S * 3, 1536)),
            "st_sc": np.ascontiguousarray(inp["state_sconv"][:, sl].reshape(L, NSEQ_S * 2, 256)),
            "st_cc": np.ascontiguousarray(inp["state_cconv"][:, sl].reshape(L, NSEQ_S * 30, 256)),
        })
    res = run_bass_kernel_spmd(nc, in_maps, core_ids=list(range(NCORES)))
    R = res.results
    y_prompt = np.stack([R[b]["yout"][:SEQ] for b in range(NCORES)], axis=0)
    y_sample = np.concatenate([R[b]["yout"][SEQ:].reshape(NSEQ_S, TS, D) for b in range(NCORES)], axis=0)
    p_delta = np.stack([R[b]["p_delta"] for b in range(NCORES)], axis=1)
    p_qkv = np.stack([R[b]["p_qkv"] for b in range(NCORES)], axis=1)
    p_sconv = np.stack([R[b]["p_sconv"] for b in range(NCORES)], axis=1)
    p_cconv = np.stack([R[b]["p_cconv"] for b in range(NCORES)], axis=1)
    s_delta = np.concatenate([R[b]["s_delta"] for b in range(NCORES)], axis=1)
    s_qkv = np.concatenate([R[b]["s_qkv"].reshape(L, NSEQ_S, 3, 1536) for b in range(NCORES)], axis=1)
    s_sconv = np.concatenate([R[b]["s_sconv"].reshape(L, NSEQ_S, 2, 256) for b in range(NCORES)], axis=1)
    s_cconv = np.concatenate([R[b]["s_cconv"].reshape(L, NSEQ_S, 30, 256) for b in range(NCORES)], axis=1)
    f = np.float32
    return (y_prompt.astype(f), y_sample.astype(f), p_delta.astype(f), p_qkv.astype(f), p_sconv.astype(f), p_cconv.astype(f),
            s_delta.astype(f), s_qkv.astype(f), s_sconv.astype(f), s_cconv.astype(f))
```

```python
import contextlib
import math
import numpy as np
import concourse.bass as bass
import concourse.mybir as mybir
from concourse.bass_utils import run_bass_kernel_spmd

F32 = mybir.dt.float32
BF16 = mybir.dt.bfloat16
AF = mybir.ActivationFunctionType
ALU = mybir.AluOpType
AX = mybir.AxisListType

NCORES = 8
L = 2
D = 1024
SEQ = 2048
NSEQ_S = 16
TS = 4
NTOK = SEQ + NSEQ_S * TS
EPS = 1e-6
NEG = -30000.0
NGROUPS = 2
NFILL = 0
CSTEPS = 5
CLEAD = 3
NSLOT = 10
ESTEPS = 1

C_QKV, C_GD, C_SB, C_SC, C_SX, C_GS, C_GA, C_GB, C_GC = 0, 1536, 2048, 2304, 2560, 2816, 3072, 3328, 3584
NCOL = 3840

P_NG, P_CQ, P_CS, P_CC, P_CB, P_CLG, P_CLB, P_DNG, P_ALOG, P_DTB = 0, 8, 56, 62, 124, 126, 128, 130, 131, 135
P_PER = 139
P_FNG = L * P_PER
NP_ = P_FNG + D

F_ID, F_TRIP, F_TRIS, F_LMP, F_LMS, F_ONES, F_SEL, F_LASTP, F_LASTS, F_SEQM, F_MHALF, F_LN05, F_NLS = (
    0, 128, 256, 384, 512, 640, 768, 1152, 1153, 1169, 1185, 1186, 1187)
F_EPS = 1188
NCF = 1192
B_ID, B_NSP, B_NSS, B_NIP, B_NIS, B_D16, B_L32, B_L64, B_L128, B_ONES = (
    0, 128, 256, 384, 512, 640, 768, 896, 1024, 1152)
B_SELNB, B_SELC = 1280, 1408
NCB = 1536


class Res:
    __slots__ = ("name", "w", "rs")

    def __init__(self, name):
        self.name = name
        self.w = None
        self.rs = []


class T:
    __slots__ = ("h", "res", "psum")

    def __init__(self, h, name, psum=False):
        self.h = h
        self.res = Res(name)
        self.psum = psum

    def __getitem__(self, k):
        return self.h[k]


class G4:
    __slots__ = ("h", "t")

    def __init__(self, h, name):
        self.h = h
        self.t = [T(h, "%s_%d" % (name, i)) for i in range(4)]

    def __getitem__(self, k):
        return self.h[k]

    def g(self, hs):
        return [self.t[h] for h in hs]


def _flat(xs):
    out = []
    for x in xs:
        if isinstance(x, G4):
            out.extend(x.t)
        elif isinstance(x, (list, tuple)):
            out.extend(_flat(x))
        else:
            out.append(x)
    return out


class Instr:
    __slots__ = ("eng", "meth", "kw", "deps", "dma", "sig", "val", "semkey", "fill")

    def __init__(self, eng, meth, kw, dma, semkey):
        self.eng = eng
        self.meth = meth
        self.kw = kw
        self.deps = []
        self.dma = dma
        self.sig = dma
        self.val = None
        self.semkey = semkey
        self.fill = 0


class Prog:
    ENG = ("pe", "act", "dve", "pool", "sp")

    def __init__(self, nc):
        self.nc = nc
        self.lists = {e: [] for e in self.ENG}
        self.fill = 0
        self.fill_kw = None

    def op(self, eng, meth, kw, r=(), w=(), dma=False, semkey=None):
        ins = Instr(eng, meth, kw, dma, semkey)
        if eng == "pe":
            ins.fill = self.fill
        r = _flat(r)
        w = _flat(w)
        w = list(w) + [t for t in r if isinstance(t, T) and t.psum]
        deps = []
        for t in r:
            x = t.res if isinstance(t, T) else t
            if x.w is not None:
                deps.append(x.w)
        for t in w:
            x = t.res if isinstance(t, T) else t
            if x.w is not None:
                deps.append(x.w)
            last_by_eng = {}
            for rd in x.rs:
                if rd.dma or rd.eng in ("pool", "sp"):
                    deps.append(rd)
                else:
                    last_by_eng[rd.eng] = rd
            deps.extend(last_by_eng.values())
        seen = set()
        for d in deps:
            if id(d) in seen or d is ins:
                continue
            seen.add(id(d))
            if d.eng == "pe" and eng == "pe" and not d.dma and not dma:
                continue
            ins.deps.append(d)
            d.sig = True
        for t in r:
            x = t.res if isinstance(t, T) else t
            x.rs.append(ins)
        for t in w:
            x = t.res if isinstance(t, T) else t
            x.w = ins
            x.rs = []
        self.lists[eng].append(ins)
        return ins

    def emit(self):
        nc = self.nc
        pos = {}
        for e in self.ENG:
            for i, ins in enumerate(self.lists[e]):
                pos[id(ins)] = i
                if not ins.dma:
                    ins.sig = False
        for e in self.ENG:
            waited_idx = {}
            for ins in self.lists[e]:
                best = {}
                keep = []
                for d in ins.deps:
                    if d.dma:
                        keep.append(d)
                        continue
                    if waited_idx.get(d.eng, -1) >= pos[id(d)]:
                        continue
                    if d.eng not in best or pos[id(best[d.eng])] < pos[id(d)]:
                        best[d.eng] = d
                for pe_, d in best.items():
                    waited_idx[pe_] = pos[id(d)]
                    d.sig = True
                    keep.append(d)
                ins.deps = keep
        semkeys = []
        counts = {}
        for e in self.ENG:
            for ins in self.lists[e]:
                if not ins.sig:
                    continue
                k = ("dma", ins.semkey) if ins.dma else ("eng", e)
                if k not in counts:
                    counts[k] = 0
                    semkeys.append(k)
                counts[k] += 16 if ins.dma else 1
                ins.val = (k, counts[k])
        assert len(semkeys) <= 140, len(semkeys)
        with contextlib.ExitStack() as st:
            sems = {}
            for i, k in enumerate(semkeys):
                sems[k] = st.enter_context(nc.semaphore("s%d" % i))
            block = st.enter_context(nc.Block())

            def mk(e):
                def body(engine):
                    waited = {}
                    for ins in self.lists[e]:
                        need = [d.val for d in ins.deps if waited.get(d.val[0], 0) < d.val[1]]
                        if need and ins.fill and self.fill_kw is not None:
                            for _ in range(ins.fill):
                                engine.matmul(**self.fill_kw)
                        for d in ins.deps:
                            k, v = d.val
                            if waited.get(k, 0) < v:
                                engine.wait_ge(sems[k], v)
                                waited[k] = v
                        h = getattr(engine, ins.meth)(**ins.kw)
                        if ins.sig:
                            k, v = ins.val
                            h.then_inc(sems[k], 16 if ins.dma else 1)
                    if e == "sp":
                        for k, v in counts.items():
                            if k[0] == "dma" and waited.get(k, 0) < v:
                                engine.wait_ge(sems[k], v)
                return body
            block.tensor(mk("pe"))
            block.scalar(mk("act"))
            block.vector(mk("dve"))
            block.gpsimd(mk("pool"))
            block.sync(mk("sp"))
        return len(semkeys)


def make_consts():
    cf = np.zeros((128, NCF), np.float32)
    i = np.arange(128)
    cf[:, F_ID:F_ID + 128] = np.eye(128)
    cf[:, F_TRIP:F_TRIP + 128] = (i[:, None] <= i[None, :])
    s64 = np.arange(64)
    same = (s64[:, None] // TS) == (s64[None, :] // TS)
    cf[:64, F_TRIS:F_TRIS + 64] = same & (s64[:, None] <= s64[None, :])
    cf[:, F_LMP:F_LMP + 128] = (i[:, None] == 127)
    cf[:64, F_LMS:F_LMS + 64] = (s64[:, None] == (s64[None, :] // TS) * TS + TS - 1)
    cf[:, F_ONES:F_ONES + 128] = 1.0
    for r in range(3):
        cf[r, F_SEL + r * 128:F_SEL + (r + 1) * 128] = 1.0
    cf[127, F_LASTP] = 1.0
    for q in range(NSEQ_S):
        cf[TS * q + TS - 1, F_LASTS + q] = 1.0
        cf[TS * q:TS * q + TS, F_SEQM + q] = 1.0
    cf[:, F_MHALF] = -0.5
    cf[:, F_LN05] = math.log(0.5)
    cf[:, F_NLS] = -math.log(math.sqrt(128.0))
    cf[:, F_EPS] = EPS
    cb = np.zeros((128, NCB), np.float32)
    cb[:, B_ID:B_ID + 128] = np.eye(128)
    cb[:, B_NSP:B_NSP + 128] = np.where(i[:, None] > i[None, :], 0.0, NEG)
    cb[:, B_NSS:B_NSS + 128] = NEG
    cb[:64, B_NSS:B_NSS + 64] = np.where(same & (s64[:, None] > s64[None, :]), 0.0, NEG)
    cb[:, B_NIP:B_NIP + 128] = np.where(i[None, :] >= i[:, None], 0.0, NEG)
    cb[:, B_NIS:B_NIS + 128] = NEG
    cb[:64, B_NIS:B_NIS + 64] = np.where(same & (s64[None, :] >= s64[:, None]), 0.0, NEG)
    cb[:, B_D16:B_D16 + 128] = (i[:, None] // 16 == i[None, :] // 16)
    cb[:, B_L32:B_L32 + 128] = (i[:, None] // 32 == i[None, :] // 32) & (i[:, None] // 16 != i[None, :] // 16)
    cb[:, B_L64:B_L64 + 128] = (i[:, None] // 64 == i[None, :] // 64) & (i[:, None] // 32 != i[None, :] // 32)
    cb[:, B_L128:B_L128 + 128] = (i[:, None] // 64 != i[None, :] // 64)
    cb[:, B_ONES:B_ONES + 128] = 1.0
    for r_ in (1, 4, 7):
        cb[r_, B_SELNB:B_SELNB + 128] = 1.0
    for r_ in (2, 5, 8):
        cb[r_, B_SELC:B_SELC + 128] = 1.0
    return cf, cb


def pack_params(inp):
    p = np.zeros((128, NP_), np.float32)
    for l in range(L):
        o = l * P_PER
        p[:, o + P_NG:o + P_NG + 8] = inp["norm_g"][l].reshape(8, 128).T
        cq = inp["conv_qkv_w"][l].reshape(4, 12, 128)
        p[:, o + P_CQ:o + P_CQ + 48] = cq.transpose(2, 1, 0).reshape(128, 48)
        cs = inp["sconv_w"][l].reshape(3, 2, 128)
        p[:, o + P_CS:o + P_CS + 6] = cs.transpose(2, 1, 0).reshape(128, 6)
        cc = inp["cconv_w"][l].reshape(31, 2, 128)
        p[:, o + P_CC:o + P_CC + 62] = cc.transpose(2, 1, 0).reshape(128, 62)
        p[:, o + P_CB:o + P_CB + 2] = inp["cconv_b"][l].reshape(2, 128).T
        p[:, o + P_CLG:o + P_CLG + 2] = inp["cln_g"][l].reshape(2, 128).T
        p[:, o + P_CLB:o + P_CLB + 2] = inp["cln_b"][l].reshape(2, 128).T
        p[:, o + P_DNG] = inp["delta_norm_g"][l]
        p[:, o + P_ALOG:o + P_ALOG + 4] = inp["a_log"][l][None, :]
        p[:, o + P_DTB:o + P_DTB + 4] = inp["dt_bias"][l][None, :]
    p[:, P_FNG:P_FNG + D] = inp["final_norm_g"][None, :]
    return p


def build_nc(nlayers=L, ntiles=5):
    nc = bass.Bass("TRN2", target_bir_lowering=False)

    def din(name, shape):
        return nc.dram_tensor(name, list(shape), F32, kind="ExternalInput").ap()

    def dout(name, shape):
        return nc.dram_tensor(name, list(shape), F32, kind="ExternalOutput").ap()

    xin = din("xin", [NTOK, D])
    w_in = din("w_in", [L, D, NCOL])
    w_bd = din("w_bd", [L, D, 8])
    w_out = din("w_out", [L, D, D])
    prm_d = din("prm", [128, NP_])
    cf_d = din("cf", [128, NCF])
    cb_d = din("cb", [128, NCB])
    st_delta = din("st_delta", [L, NSEQ_S, 4, 128, 128])
    st_qkv = din("st_qkv", [L, NSEQ_S * 3, 1536])
    st_sc = din("st_sc", [L, NSEQ_S * 2, 256])
    st_cc = din("st_cc", [L, NSEQ_S * 30, 256])
    y1 = nc.dram_tensor("y1", [NTOK, D], F32, kind="Internal").ap()
    yout = dout("yout", [NTOK, D])
    p_delta = dout("p_delta", [L, 4, 128, 128])
    p_qkv = dout("p_qkv", [L, 3, 1536])
    p_sconv = dout("p_sconv", [L, 2, 256])
    p_cconv = dout("p_cconv", [L, 30, 256])
    s_delta = dout("s_delta", [L, NSEQ_S, 4, 128, 128])
    s_qkv = dout("s_qkv", [L, NSEQ_S * 3, 1536])
    s_sconv = dout("s_sconv", [L, NSEQ_S * 2, 256])
    s_cconv = dout("s_cconv", [L, NSEQ_S * 30, 256])

    st = contextlib.ExitStack()
    with st:
        def sb4(name, shape, dt=F32):
            return G4(st.enter_context(nc.sbuf_tensor("sb_" + name, list(shape), dt)), name)

        evc = [0]

        def ev():
            evc[0] += 1
            return "act" if evc[0] % 2 else "dve"

        def sb(name, shape, dt=F32):
            return T(st.enter_context(nc.sbuf_tensor("sb_" + name, list(shape), dt)), name)

        P = Prog(nc)
        op = P.op

        Wg = [sb("wg%d" % g, [128, 8, n], BF16) for g, n in enumerate([1536, 512, 1024, 768])]
        Wgoff = [0, 1536, 2048, 3072]
        Wbd = sb("wbd", [128, 8, 8], BF16)
        Wo = sb("wo", [128, 8, D], BF16)
        prm = sb("prm", [128, NP_])
        cf = sb("cf", [128, NCF])
        cb = sb("cb", [128, NCB], BF16)
        dcr = [sb("dcr%d" % i, [128, 128], BF16) for i in range(NSLOT)]
        dcc = [0]
        xb = [sb("xb%d" % i, [128, D]) for i in range(2)]
        xh = sb("xh", [128, D], BF16)
        junk = xh
        hT = sb("hT", [128, 8, 512], BF16)
        oT = sb4("oT", [128, 8, 512], BF16)
        rr = [sb("rr%d" % i, [128, 528], BF16) for i in range(3)]
        raw_s = sb("raw_s", [128, 12, NSEQ_S, 7], BF16)
        rawlast = sb("rawlast", [128, 12, 4])
        hist = sb("hist", [128, 12, 4], BF16)
        qkvT = sb("qkvT", [128, 12, 512], BF16)
        gd2 = sb("gd2", [128, 4, 512], BF16)
        sqs = sb("sqs", [128, 512], BF16)
        sqs2 = sb("sqs2", [128, 512], BF16)
        hsb = sb("hsb", [128, 2, 520])
        ub = sb("ub", [128, 2, 544])
        ubf = sb("ubf", [128, 2, 544], BF16)
        stg = xb[0]
        stg2 = xb[1]
        S = sb4("S", [128, 512])
        Sbf = sb4("Sbf", [128, 512], BF16)
        ss = sb("ss", [128, 8])
        rs = sb("rs", [128, 8])
        negA = sb("negA", [128, L * 4])
        hdng = sb("hdng", [128, L])
        bdS = sb("bdS", [128, 4, 8])
        zz = sb("zz", [128, 4, 8])
        az = sb("az", [128, 4, 8])
        lnin = sb("lnin", [128, 4, 16])
        lnout = sb("lnout", [128, 4, 16])
        spz = sb("spz", [128, 4, 8])
        gg = sb("gg", [128, 4, 4])
        gc = sb("gc", [128, 4, 4])
        hk = sb("hk", [128, 4, 4])
        tm1 = sb("tm1", [128, 4, 4])
        tm2 = sb("tm2", [128, 4, 4])
        R3 = sb("R3", [128, 4, 4, 3])
        ea = sb("ea", [128, 4, 4])
        evb = sb("evb", [128, 4, 4])
        ec = sb("ec", [128, 4, 4])
        es = sb("es", [128, 4, 4])
        glt = sb("glt", [128, 4, 4])
        gsel = sb("gsel", [128, 64])
        dec = sb("dec", [128, 64])
        Rw = sb4("Rw0", [9, 512], BF16)
        R9 = sb("R9", [128, 4, 4, 9], BF16)
        R3r = sb("R3r", [128, 4, 4, 3])
        ssqo = sb4("ssqo", [128, 4])
        rso = sb4("rso", [128, 4])
        lst = sb("lst", [128, 4, 2])
        lvr = sb("lvr", [128, 4])
        LP = sb("LP", [128, 4, 2])
        hcl = sb("hcl", [128, L * 4])
        hcc = sb("hcc", [128, L * 62])
        ssf = sb("ssf", [128, 2])
        rsf = sb("rsf", [128, 2])
        EA = sb4("EA", [128, 512])
        EM = sb4("EM", [128, 512])
        t1 = sb4("t1", [128, 512])
        ot = sb4("ot", [128, 512])
        osq = sb4("osq", [128, 512])
        o1T = ot
        bfn = ["Am", "Bm", "Mt", "Kw", "Ks", "Vb", "Pa", "Pb", "Qa", "Qb", "Ta", "Tb", "Ua", "Ub", "Xm", "Ao", "th", "th2", "Mt2", "Ks2", "Vb2"]
        thc = [0]
        bt = {n: sb4(n, [128, 512], BF16) for n in bfn}
        bt["Bo"] = bt["Ao"]
        bt["nwT"] = bt["Qb"]
        bt["on"] = bt["Qa"]
        bt["vnew"] = bt["Xm"]
        bt["vnT"] = bt["Pa"]
        bt["Ksm"] = bt["Pb"]
        S0 = [EA, EM]
        SN = [t1, osq]
        S0b = [bt["Ao"], bt["Pa"], bt["Pb"]]
        MtB = [bt["Mt"], bt["Mt2"]]
        KsB = [bt["Ks"], bt["Ks2"]]
        VbB = [bt["Vb"], bt["Vb2"]]
        mu, var = ot, osq
        csb = sb("csb", [128, 512])
        ys = sb("ys", [128, 512])
        g2 = sb("g2", [128, 512])
        LR = csb
        yc = sb("yc", [128, 2, 512])
        ysq = [sb("ysq0", [128, 512], BF16), sb("ysq1", [128, 512], BF16)]
        rawf_s = yc

        banks = [T(st.enter_context(nc.psum_tensor("pb%d" % i, [128, 512], F32)), "pb%d" % i, psum=True) for i in range(8)]
        bpool = {"all": list(range(7)), "D": [0, 1, 2, 3], "C": [4, 5, 6]}
        P.fill_kw = dict(out=banks[7][:, 0:128], lhsT=cb[:, B_ID:B_ID + 128], rhs=cb[:, B_ID:B_ID + 128], start=True, stop=True)
        bcur = ["all"]
        bidx = {"all": 0, "D": 0, "C": 0}

        def bank():
            p_ = bcur[0]
            lst_ = bpool[p_]
            b = banks[lst_[bidx[p_] % len(lst_)]]
            bidx[p_] += 1
            return b

        def bfv(b):
            return b.h[:, :].bitcast(BF16)

        y1res = [Res("y1_%d" % i) for i in range(17)]

        def pc(o, n=1):
            return prm[:, o:o + n]

        op("sp", "dma_start", dict(out=prm[:, :], in_=prm_d), w=[prm], dma=True, semkey="prm")
        op("sp", "dma_start", dict(out=cf[:, :], in_=cf_d), w=[cf], dma=True, semkey="cf")
        op("pool", "dma_start", dict(out=cb[:, :], in_=cb_d), w=[cb], dma=True, semkey="cb")
        identf = cf[:, F_ID:F_ID + 128]
        identb = cb[:, B_ID:B_ID + 128]

        def load_wgroup(l, g):
            if g < 4:
                n = Wg[g].h.shape[2]
                src = w_in[l, :, Wgoff[g]:Wgoff[g] + n].rearrange("(k p) n -> p k n", p=128)
                op("pool", "dma_start", dict(out=Wg[g][:, :, :], in_=src), w=[Wg[g]], dma=True, semkey="wg%d" % g)
                if g == 0:
                    op("pool", "dma_start", dict(out=Wbd[:, :, :], in_=w_bd[l].rearrange("(k p) n -> p k n", p=128)), w=[Wbd], dma=True, semkey="wbd")
            else:
                op("pool", "dma_start", dict(out=Wo[:, :, :], in_=w_out[l].rearrange("(k p) n -> p k n", p=128)), w=[Wo], dma=True, semkey="wo")

        def load_weights(l):
            for g in range(4):
                n = Wg[g].h.shape[2]
                src = w_in[l, :, Wgoff[g]:Wgoff[g] + n].rearrange("(k p) n -> p k n", p=128)
                op("pool", "dma_start", dict(out=Wg[g][:, :, :], in_=src), w=[Wg[g]], dma=True,
                   semkey="wg%d" % g)
                if g == 0:
                    op("pool", "dma_start", dict(out=Wbd[:, :, :], in_=w_bd[l].rearrange("(k p) n -> p k n", p=128)),
                       w=[Wbd], dma=True, semkey="wbd")
            op("pool", "dma_start", dict(out=Wo[:, :, :], in_=w_out[l].rearrange("(k p) n -> p k n", p=128)),
               w=[Wo], dma=True, semkey="wo")

        load_weights(0)
        for l in range(L):
            o = l * P_PER
            op("act", "activation", dict(out=negA[:, l * 4:l * 4 + 4], in_=pc(o + P_ALOG, 4), func=AF.Exp),
               r=[prm], w=[negA])
            op("dve", "tensor_scalar", dict(out=hdng[:, l:l + 1], in0=pc(o + P_DNG), scalar1=0.5, scalar2=None,
                                                          op0=ALU.mult), r=[prm], w=[hdng])
        op("dve", "tensor_scalar", dict(out=negA[:, :], in0=negA[:, :], scalar1=-1.0, scalar2=None, op0=ALU.mult),
           r=[negA], w=[negA])
        for l in range(L):
            o = l * P_PER
            op("dve", "tensor_scalar", dict(out=hcc[:, l * 62:(l + 1) * 62], in0=pc(o + P_CC, 62), scalar1=0.5, scalar2=None, op0=ALU.mult),
               r=[prm], w=[hcc])
            op("dve", "tensor_scalar", dict(out=hcl[:, l * 4:l * 4 + 2], in0=pc(o + P_CLG, 2), scalar1=0.5, scalar2=None, op0=ALU.mult),
               r=[prm], w=[hcl])
            op("dve", "tensor_scalar", dict(out=hcl[:, l * 4 + 2:l * 4 + 4], in0=pc(o + P_CLB, 2), scalar1=0.5, scalar2=None, op0=ALU.mult),
               r=[prm], w=[hcl])

        s1done = set()

        def stage1_gen(l2, ti2):
            s1done.add((l2, ti2))
            sample2 = (ti2 == 4)
            PT2 = 64 if sample2 else 128
            nsub2 = 1 if sample2 else 4
            po2 = l2 * P_PER
            src2 = xin if l2 == 0 else y1
            ycf_ = yc[:, :, :].rearrange("p a b -> p (a b)")
            for s in range(nsub2):
                r0 = ti2 * 512 + s * 128
                x_ = ycf_
                yres = y1res[ti2 * 4 + s]
                op("act", "dma_start", dict(out=x_[:PT2, :], in_=src2[r0:r0 + PT2, :]),
                   r=([yres] if l2 > 0 else []), w=[yc], dma=True, semkey="xf")
                yield
                op("act", "activation", dict(out=xh[:PT2, :], in_=x_[:PT2, :], func=AF.Square, accum_out=ss[:PT2, s:s + 1]), r=[yc], w=[xh, ss])
                op("act", "activation", dict(out=rs[:PT2, s:s + 1], in_=ss[:PT2, s:s + 1], func=AF.Ln, scale=1.0 / D,
                                             bias=cf[:PT2, F_EPS:F_EPS + 1]), r=[ss, cf], w=[rs])
                op("act", "activation", dict(out=rs[:PT2, s:s + 1], in_=rs[:PT2, s:s + 1], func=AF.Exp, scale=-0.5), r=[rs], w=[rs])
                yield
                op("act", "activation", dict(out=xh[:PT2, :], in_=x_[:PT2, :], func=AF.Copy, scale=rs[:PT2, s:s + 1]), r=[yc, rs], w=[xh])
                yield
                yield
                yield
                b = banks[7]
                for kc in range(8):
                    op("pe", "transpose", dict(out=bfv(b)[:, kc * 128:kc * 128 + PT2], in_=xh[:PT2, kc * 128:(kc + 1) * 128],
                                               identity=identb[:PT2, :PT2]), r=[xh, cb], w=[b])
                op("dve", "tensor_tensor", dict(out=hT[:, :, s * 128:s * 128 + PT2], in0=bfv(b).rearrange("p (k t) -> p k t", k=8)[:, :, :PT2],
                                                in1=pc(po2 + P_NG, 8).unsqueeze(2).to_broadcast([128, 8, PT2]), op=ALU.mult), r=[b, prm], w=[hT])
                yield

        def layer(l):
            po = l * P_PER
            last_layer = (l == nlayers - 1)
            op("pool", "memset", dict(ap=hist[:, :, :], constant=0.0), w=[hist])
            op("pool", "memset", dict(ap=hsb[:, :, :], constant=0.0), w=[hsb])
            op("pool", "memset", dict(ap=ub[:, :, :], constant=0.0), w=[ub])
            op("pool", "memset", dict(ap=S[:, :], constant=0.0), w=[S])
            op("pool", "memset", dict(ap=Sbf[:, :], constant=0.0), w=[Sbf])

            src_x = xin if l == 0 else y1

            for ti in range(ntiles):
                sample = (ti == 4)
                tok0 = ti * 512
                TW = 64 if sample else 512
                nsub = 1 if sample else 4
                PT = 64 if sample else 128
                nseq = NSEQ_S if sample else 1
                TT = TS if sample else 512
                lastp = (ti == 3)

                def v3(ap2):
                    return ap2.rearrange("p (q t) -> p q t", q=nseq)

                if (l, ti) not in s1done:
                    for _ in stage1_gen(l, ti):
                        pass

                def proj(col):
                    g = max(i for i in range(4) if Wgoff[i] <= col)
                    lc = col - Wgoff[g]
                    b = bank()
                    for kc in range(8):
                        op("pe", "matmul", dict(out=b[:, :TW], lhsT=Wg[g][:, kc, lc:lc + 128],
                                                                              rhs=hT[:, kc, :TW], start=(kc == 0), stop=(kc == 7)),
                           r=[Wg[g], hT], w=[b])
                    return b

                def proj_y(col):
                    g = max(i for i in range(4) if Wgoff[i] <= col)
                    lc = col - Wgoff[g]
                    b = bank()
                    for kc in range(8):
                        op("pe", "matmul", dict(out=b[:, :TW], lhsT=Wg[g][:, kc, lc:lc + 128], rhs=hT[:, kc, :TW], start=(kc == 0), stop=(kc == 7)),
                           r=[Wg[g], hT], w=[b])
                        if kc % 2 == 1 and kc < 7:
                            yield
                    return b

                def silu2(b, out_ap, outT, eng="dve"):
                    thc[0] += 1
                    th = bt["th"] if thc[0] % 2 else bt["th2"]
                    op("act", "activation", dict(out=th[:, :TW], in_=b[:, :TW], func=AF.Tanh, scale=0.5), r=[b], w=[th])
                    op(eng, "scalar_tensor_tensor", dict(out=out_ap, in0=th[:, :TW], scalar=1.0, in1=b[:, :TW],
                                                             op0=ALU.add, op1=ALU.mult), r=[th, b], w=[outT])

                if sample:
                    op("sp", "dma_start", dict(out=stg[:48, :1024], in_=st_qkv[l][:, 0:1024]), w=[stg], dma=True, semkey="stg")
                    op("sp", "dma_start", dict(out=stg2[:48, :512], in_=st_qkv[l][:, 1024:1536]), w=[stg2], dma=True, semkey="stg2")
                    for c in range(12):
                        b = bank()
                        sg_ = stg if c < 8 else stg2
                        op("pe", "transpose", dict(out=b[:, :48], in_=sg_[:48, (c % 8) * 128:(c % 8 + 1) * 128],
                                                                   identity=identf[:48, :48]), r=[sg_, cf], w=[b])
                        op("dve", "tensor_copy", dict(out=raw_s[:, c, :, 0:3],
                                                                    in_=b[:, :48].rearrange("p (q r) -> p q r", q=NSEQ_S)),
                           r=[b], w=[raw_s])
                    op("sp", "dma_start", dict(out=stg2[:32, :256], in_=st_sc[l]), w=[stg2], dma=True, semkey="stg2")
                    for j in range(2):
                        b = bank()
                        op("pe", "transpose", dict(out=b[:, :32], in_=stg2[:32, j * 128:(j + 1) * 128],
                                                                   identity=identf[:32, :32]), r=[stg2, cf], w=[b])
                        op("dve", "tensor_copy", dict(
                            out=hsb[:, j, 0:96].rearrange("p (q t) -> p q t", q=NSEQ_S)[:, :, 0:2],
                            in_=b[:, :32].rearrange("p (q r) -> p q r", q=NSEQ_S)), r=[b], w=[hsb])
                    op("sp", "dma_start", dict(out=stg[:120, :1024].rearrange("p (q c) -> p q c", q=4),
                                                   in_=st_cc[l].rearrange("(q p) c -> p q c", p=120)),
                       w=[stg], dma=True, semkey="stg")
                    for q in range(4):
                        for j in range(2):
                            b = bank()
                            op("pe", "transpose", dict(
                                out=b[:, :120], in_=stg[:120, q * 256 + j * 128:q * 256 + (j + 1) * 128],
                                identity=identf[:120, :120]), r=[stg, cf], w=[b])
                            op("act", "activation", dict(
                                out=ub[:, j, :].rearrange("p (q t) -> p q t", q=NSEQ_S)[:, 4 * q:4 * q + 4, 0:30],
                                in_=b[:, :120].rearrange("p (q r) -> p q r", q=4), func=AF.Copy, scale=2.0), r=[b], w=[ub])

                bbd = bank()
                for s in range(nsub):
                    for kc in range(8):
                        op("pe", "matmul", dict(out=bbd[:PT, s * 8:(s + 1) * 8], lhsT=hT[:, kc, s * 128:s * 128 + PT],
                                                                  rhs=Wbd[:, kc, :], start=(kc == 0), stop=(kc == 7)),
                           r=[hT, Wbd], w=[bbd])
                op("dve", "tensor_copy", dict(out=bdS[:PT, :nsub, :], in_=bbd[:PT, :nsub * 8].rearrange("p (s c) -> p s c", c=8)),
                   r=[bbd], w=[bdS])

                NS = slice(0, nsub)
                def scal_part1():
                    NS = slice(0, nsub)
                    op("dve", "tensor_tensor", dict(out=zz[:PT, NS, 0:4], in0=bdS[:PT, NS, 4:8],
                                                        in1=pc(po + P_DTB, 4)[:PT].unsqueeze(1).to_broadcast([PT, nsub, 4]), op=ALU.add),
                       r=[bdS, prm], w=[zz])
                    op("dve", "tensor_scalar", dict(out=zz[:PT, NS, 4:8], in0=bdS[:PT, NS, 0:4], scalar1=-1.0, scalar2=None,
                                                        op0=ALU.mult), r=[bdS], w=[zz])
                    op("dve", "scalar_tensor_tensor", dict(out=az[:PT, NS, :], in0=zz[:PT, NS, :], scalar=-1.0, in1=zz[:PT, NS, :],
                                                               op0=ALU.mult, op1=ALU.max), r=[zz], w=[az])
                    op("act", "activation", dict(out=lnin[:PT, NS, 0:8], in_=az[:PT, NS, :], func=AF.Exp, scale=-1.0), r=[az], w=[lnin])
                    op("dve", "tensor_scalar", dict(out=lnin[:PT, NS, 0:8], in0=lnin[:PT, NS, 0:8], scalar1=1.0, scalar2=None,
                                                        op0=ALU.add), r=[lnin], w=[lnin])
                    op("act", "activation", dict(out=lnout[:PT, NS, 0:8], in_=lnin[:PT, NS, 0:8], func=AF.Ln), r=[lnin], w=[lnout])
                    op("dve", "scalar_tensor_tensor", dict(out=spz[:PT, NS, :], in0=zz[:PT, NS, :], scalar=0.0, in1=lnout[:PT, NS, 0:8],
                                                               op0=ALU.max, op1=ALU.add), r=[zz, lnout], w=[spz])
                    op("dve", "tensor_tensor", dict(out=gg[:PT, NS, :], in0=spz[:PT, NS, 0:4],
                                                        in1=negA[:PT, l * 4:l * 4 + 4].unsqueeze(1).to_broadcast([PT, nsub, 4]), op=ALU.mult),
                       r=[spz, negA], w=[gg])
                    tri = cf[:PT, (F_TRIS if sample else F_TRIP):(F_TRIS if sample else F_TRIP) + PT]
                    lmm = cf[:PT, (F_LMS if sample else F_LMP):(F_LMS if sample else F_LMP) + PT]
                    bg = bank()
                    for s in range(nsub):
                        op("pe", "matmul", dict(out=bg[:PT, s * 4:(s + 1) * 4], lhsT=tri, rhs=gg[:PT, s, :], start=True, stop=True),
                           r=[cf, gg], w=[bg])
                    op("dve", "tensor_copy", dict(out=gc[:PT, NS, :], in_=bg[:PT, :nsub * 4].rearrange("p (s c) -> p s c", c=4)),
                       r=[bg], w=[gc])
                    bl = bank()
                    for s in range(nsub):
                        op("pe", "matmul", dict(out=bl[:PT, s * 4:(s + 1) * 4], lhsT=lmm, rhs=gc[:PT, s, :], start=True, stop=True),
                           r=[cf, gc], w=[bl])
                    op("dve", "tensor_copy", dict(out=glt[:PT, NS, :], in_=bl[:PT, :nsub * 4].rearrange("p (s c) -> p s c", c=4)),
                       r=[bl], w=[glt])
                    op("act", "activation", dict(out=evb[:PT, NS, :], in_=spz[:PT, NS, 4:8], func=AF.Exp, scale=-1.0,
                                                     bias=cf[:PT, F_LN05:F_LN05 + 1]), r=[spz, cf], w=[evb])
                    if sample:
                        ng_ = 64
                        op("dve", "tensor_tensor", dict(
                            out=gsel[:PT, :].rearrange("p (q h) -> p q h", h=4),
                            in0=gc[:PT, 0, :].unsqueeze(1).to_broadcast([PT, NSEQ_S, 4]),
                            in1=cf[:PT, F_LASTS:F_LASTS + NSEQ_S].unsqueeze(2).to_broadcast([PT, NSEQ_S, 4]), op=ALU.mult),
                           r=[gc, cf], w=[gsel])
                    else:
                        ng_ = 16
                        op("dve", "tensor_scalar", dict(out=gsel[:, :16], in0=gc[:, :, :].rearrange("p s h -> p (s h)"),
                                                            scalar1=cf[:, F_LASTP:F_LASTP + 1], scalar2=None, op0=ALU.mult),
                           r=[gc, cf], w=[gsel])
                    bd_ = bank()
                    op("pe", "matmul", dict(out=bd_[:, :ng_], lhsT=cf[:PT, F_ONES:F_ONES + 128], rhs=gsel[:PT, :ng_], start=True, stop=True),
                       r=[cf, gsel], w=[bd_])
                    op("act", "activation", dict(out=dec[:, :ng_], in_=bd_[:, :ng_], func=AF.Exp), r=[bd_], w=[dec])


                def sumsq_block():
                    bss = bank()
                    sqb = [sqs, sqs2]

                    def sq_(c2):
                        op("pool", "tensor_tensor", dict(out=sqb[c2 % 2][:, :TW], in0=qkvT[:, c2, :TW], in1=qkvT[:, c2, :TW], op=ALU.mult),
                           r=[qkvT], w=[sqb[c2 % 2]])
                    sq_(0)
                    for c in range(8):
                        if c + 1 < 8:
                            sq_(c + 1)
                        if c % 2 == 0:
                            bgd_ = proj(C_GD + (c // 2) * 128)
                            silu2(bgd_, gd2[:, c // 2, :TW], gd2)
                        for s in range(nsub):
                            op("pe", "matmul", dict(out=bss[:PT, s * 8 + c:s * 8 + c + 1], lhsT=sqb[c % 2][:, s * 128:s * 128 + PT],
                                                    rhs=cb[:, B_ONES:B_ONES + 1], start=True, stop=True), r=[sqb[c % 2], cb], w=[bss])
                    op("dve", "tensor_scalar", dict(out=lnin[:PT, NS, 8:16], in0=bss[:PT, :nsub * 8].rearrange("p (s c) -> p s c", c=8),
                                                    scalar1=EPS, scalar2=None, op0=ALU.add), r=[bss], w=[lnin])

                def scal_part2():
                    op("act", "activation", dict(out=lnout[:PT, NS, 8:16], in_=lnin[:PT, NS, 8:16], func=AF.Ln), r=[lnin], w=[lnout])
                    op("dve", "tensor_scalar", dict(out=hk[:PT, NS, :], in0=lnout[:PT, NS, 12:16], scalar1=0.5, scalar2=None, op0=ALU.mult),
                       r=[lnout], w=[hk])
                    op("dve", "scalar_tensor_tensor", dict(out=R3[:PT, NS, :, 1], in0=gc[:PT, NS, :], scalar=-1.0, in1=hk[:PT, NS, :],
                                                               op0=ALU.mult, op1=ALU.subtract), r=[gc, hk], w=[R3])
                    op("dve", "tensor_tensor", dict(out=tm1[:PT, NS, :], in0=gc[:PT, NS, :], in1=spz[:PT, NS, 4:8], op=ALU.subtract),
                       r=[gc, spz], w=[tm1])
                    op("dve", "tensor_tensor", dict(out=R3[:PT, NS, :, 0], in0=tm1[:PT, NS, :], in1=hk[:PT, NS, :], op=ALU.subtract),
                       r=[tm1, hk], w=[R3])
                    op("dve", "tensor_scalar", dict(out=tm2[:PT, NS, :], in0=lnout[:PT, NS, 8:12], scalar1=-0.5,
                                                        scalar2=-math.log(math.sqrt(128.0)), op0=ALU.mult, op1=ALU.add), r=[lnout], w=[tm2])
                    op("dve", "tensor_tensor", dict(out=R3[:PT, NS, :, 2], in0=tm2[:PT, NS, :], in1=gc[:PT, NS, :], op=ALU.add),
                       r=[tm2, gc], w=[R3])
                    op("dve", "tensor_copy", dict(out=R9[:PT, NS, :, 0:3], in_=R3[:PT, NS, :, :]), r=[R3], w=[R9])
                    op("dve", "tensor_tensor", dict(out=R3r[:PT, NS, :, :], in0=R3[:PT, NS, :, :], in1=R9[:PT, NS, :, 0:3], op=ALU.subtract), r=[R3, R9], w=[R3r])
                    op("dve", "tensor_copy", dict(out=R9[:PT, NS, :, 3:6], in_=R3r[:PT, NS, :, :]), r=[R3r], w=[R9])
                    op("dve", "tensor_tensor", dict(out=R3r[:PT, NS, :, :], in0=R3r[:PT, NS, :, :], in1=R9[:PT, NS, :, 3:6], op=ALU.subtract), r=[R3r, R9], w=[R3r])
                    op("dve", "tensor_copy", dict(out=R9[:PT, NS, :, 6:9], in_=R3r[:PT, NS, :, :]), r=[R3r], w=[R9])
                    op("act", "activation", dict(out=ea[:PT, NS, :], in_=R3[:PT, NS, :, 0], func=AF.Exp), r=[R3], w=[ea])
                    op("act", "activation", dict(out=ec[:PT, NS, :], in_=R3[:PT, NS, :, 2], func=AF.Exp), r=[R3], w=[ec])
                    op("dve", "tensor_tensor", dict(out=tm1[:PT, NS, :], in0=glt[:PT, NS, :], in1=R3[:PT, NS, :, 1], op=ALU.add),
                       r=[glt, R3], w=[tm1])
                    op("act", "activation", dict(out=es[:PT, NS, :], in_=tm1[:PT, NS, :], func=AF.Exp), r=[tm1], w=[es])

                scal_part1()
                P.fill = NFILL

                def qkv_iter(c):
                        b = proj(C_QKV + c * 128)
                        if sample:
                            src_full = raw_s
                            op("act", "activation", dict(out=raw_s[:, c, :, 3:7], in_=v3(b[:, :TW]), func=AF.Copy),
                               r=[b], w=[raw_s])
                            op("dve", "tensor_copy", dict(out=rawf_s[:, :, :].rearrange("p a b -> p (a b)")[:, c * 64:(c + 1) * 64], in_=b[:, :TW]), r=[b], w=[rawf_s])
                            rhs = [raw_s[:, c, :, k:k + TS] for k in range(4)]
                            rres = raw_s
                        else:
                            r_ = rr[c % 3]
                            op("act", "activation", dict(out=r_[:, 3:515], in_=b[:, :TW], func=AF.Copy), r=[b], w=[r_])
                            op("dve", "tensor_copy", dict(out=r_[:, 0:3], in_=hist[:, c, 0:3]), r=[hist], w=[r_])
                            if lastp:
                                op("dve", "tensor_copy", dict(out=rawlast[:, c, 0:3], in_=b[:, 509:512]), r=[b], w=[rawlast])
                            rhs = [r_[:, k:k + 512] for k in range(4)]
                            rres = r_
                        qslots = []
                        for k in range(4):
                            dslot = dcr[dcc[0] % NSLOT]
                            dcc[0] += 1
                            qslots.append(dslot)
                            if ti > 0 and not sample:
                                op("pool", "tensor_scalar", dict(out=dslot[:, :], in0=identb, scalar1=pc(po + P_CQ + c * 4 + k), scalar2=1.0,
                                                                 op0=ALU.mult, op1=ALU.mult), r=[cb, prm], w=[dslot])
                            elif dcc[0] % 2:
                                op("act", "activation", dict(out=dslot[:, :], in_=identb, func=AF.Copy, scale=pc(po + P_CQ + c * 4 + k)),
                                   r=[cb, prm], w=[dslot])
                            else:
                                op("dve", "tensor_scalar", dict(out=dslot[:, :], in0=identb, scalar1=pc(po + P_CQ + c * 4 + k), scalar2=None,
                                                                op0=ALU.mult), r=[cb, prm], w=[dslot])
                        yield
                        b2 = bank()
                        for k in range(4):
                            dslot = qslots[k]
                            op("pe", "matmul", dict(out=(v3(b2[:, :TW]) if sample else b2[:, :TW]), lhsT=dslot[:, :], rhs=rhs[k],
                                                    start=(k == 0), stop=(k == 3)), r=[dslot, rres], w=[b2])
                        if not sample:
                            op("dve", "tensor_copy", dict(out=hist[:, c, 0:3], in_=r_[:, 512:515]), r=[r_], w=[hist])
                        silu2(b2, qkvT[:, c, :TW], qkvT)

                pend_ = None
                for c in range(12):
                    g_it = qkv_iter(c)
                    next(g_it)
                    if pend_ is not None:
                        for _ in pend_:
                            pass
                    pend_ = g_it
                for _ in pend_:
                    pass
                if sample and l + 1 < nlayers:
                    load_wgroup(l + 1, 0)

                if lastp or sample:
                    nr = 48 if sample else 3
                    bks = [bank(), bank(), bank()]
                    for c in range(12):
                        bb_ = bks[c // 4]
                        if sample:
                            op("pool", "tensor_copy", dict(out=csb[:, 0:48].rearrange("p (q t) -> p q t", q=NSEQ_S),
                                                           in_=rawf_s[:, :, :].rearrange("p a b -> p (a b)")[:, c * 64:(c + 1) * 64].rearrange("p (q t) -> p q t", q=NSEQ_S)[:, :, 1:4]),
                               r=[rawf_s], w=[csb])
                            src_, sres = csb[:, 0:48], csb
                        else:
                            src_, sres = rawlast[:, c, 0:3], rawlast
                        op("pe", "transpose", dict(out=bb_[:nr, (c % 4) * 128:(c % 4 + 1) * 128], in_=src_, identity=identf),
                           r=[sres, cf], w=[bb_])
                    op("dve", "tensor_copy", dict(out=stg[:nr, 0:512], in_=bks[0][:nr, :]), r=[bks[0]], w=[stg])
                    op("dve", "tensor_copy", dict(out=stg[:nr, 512:1024], in_=bks[1][:nr, :]), r=[bks[1]], w=[stg])
                    op("dve", "tensor_copy", dict(out=stg2[:nr, 0:512], in_=bks[2][:nr, :]), r=[bks[2]], w=[stg2])
                    dst = s_qkv[l] if sample else p_qkv[l]
                    op("sp", "dma_start", dict(out=dst[:, 0:1024], in_=stg[:nr, :1024]), r=[stg], dma=True, semkey="stg")
                    op("sp", "dma_start", dict(out=dst[:, 1024:1536], in_=stg2[:nr, :512]), r=[stg2], dma=True, semkey="stg2")

                sumsq_block()
                P.fill = 0
                scal_part2()

                if sample and l + 1 < nlayers:
                    load_wgroup(l + 1, 1)

                def branch_gen():
                    if sample:
                        def hs_v(j, k0, k1):
                            return hsb[:, j, 0:96].rearrange("p (q t) -> p q t", q=NSEQ_S)[:, :, k0:k1]
                    else:
                        def hs_v(j, k0, k1):
                            return hsb[:, j, k0:k1].unsqueeze(1)
                    for j in range(2):
                        bC = yield from proj_y(C_SC + j * 128)
                        op("act", "activation", dict(out=csb[:, :TW], in_=bC[:, :TW], func=AF.Copy), r=[bC], w=[csb])
                        yield
                        bX = yield from proj_y(C_SX + j * 128)
                        op("dve", "tensor_tensor", dict(out=hs_v(j, 2, 2 + TT), in0=v3(bX[:, :TW]), in1=v3(csb[:, :TW]), op=ALU.mult),
                           r=[bX, csb], w=[hsb])
                        yield
                        for k in range(3):
                            wk = pc(po + P_CS + j * 3 + k)
                            if k == 0:
                                op("dve", "tensor_scalar", dict(out=v3(ys[:, :TW]), in0=hs_v(j, 0, TT), scalar1=wk, scalar2=None, op0=ALU.mult),
                                   r=[hsb, prm], w=[ys])
                                yield
                            else:
                                op("dve", "scalar_tensor_tensor", dict(out=v3(ys[:, :TW]), in0=hs_v(j, k, k + TT), scalar=wk,
                                                                                            in1=v3(ys[:, :TW]), op0=ALU.mult, op1=ALU.add),
                                   r=[hsb, prm, ys], w=[ys])
                                yield
                        bGs = yield from proj_y(C_GS + j * 128)
                        silu2(bGs, g2[:, :TW], g2)
                        yield
                        op("pool", "tensor_tensor", dict(out=g2[:, :TW], in0=g2[:, :TW], in1=ys[:, :TW], op=ALU.mult), r=[g2, ys], w=[g2])
                        yield
                        bB_ = yield from proj_y(C_SB + j * 128)
                        op("dve", "scalar_tensor_tensor", dict(out=oT[:, 4 + j, :TW], in0=bB_[:, :TW], scalar=0.5, in1=g2[:, :TW],
                                                                                 op0=ALU.mult, op1=ALU.mult), r=[bB_, g2], w=[oT])
                        yield
                        if not sample:
                            if lastp:
                                pass
                            else:
                                op("pool", "tensor_copy", dict(out=hsb[:, j, 0:2], in_=hsb[:, j, 512:514]), r=[hsb], w=[hsb])
                                yield
                    if sample and l + 1 < nlayers:
                        load_wgroup(l + 1, 2)
                    if lastp or sample:
                        nr = 32 if sample else 2
                        b = bank()
                        for j in range(2):
                            if sample:
                                op("pool", "tensor_copy", dict(out=csb[:, j * 32:(j + 1) * 32].rearrange("p (q t) -> p q t", q=NSEQ_S),
                                                                        in_=hs_v(j, 4, 6)), r=[hsb], w=[csb])
                                yield
                                src_ = csb[:, j * 32:(j + 1) * 32]
                                sres = csb
                            else:
                                src_ = hsb[:, j, 512:514]
                                sres = hsb
                            op("pe", "transpose", dict(out=b[:nr, j * 128:(j + 1) * 128], in_=src_, identity=identf),
                               r=[sres, cf], w=[b])
                            yield
                        op("dve", "tensor_copy", dict(out=stg2[:nr, :256], in_=b[:nr, :256]), r=[b], w=[stg2])
                        yield
                        dst = s_sconv[l] if sample else p_sconv[l]
                        op("sp", "dma_start", dict(out=dst, in_=stg2[:nr, :256]), r=[stg2], dma=True, semkey="stg2")
                        yield

                    if sample:
                        def u_v(t_, j, k0, k1):
                            return t_[:, j, :].rearrange("p (q t) -> p q t", q=NSEQ_S)[:, :, k0:k1]
                    else:
                        def u_v(t_, j, k0, k1):
                            return t_[:, j, k0:k1].unsqueeze(1)
                    for j in range(2):
                        bGb = yield from proj_y(C_GB + j * 128)
                        thc[0] += 1
                        th = bt["th"] if thc[0] % 2 else bt["th2"]
                        op("act", "activation", dict(out=th[:, :TW], in_=bGb[:, :TW], func=AF.Tanh, scale=0.5), r=[bGb], w=[th])
                        yield
                        bGa = yield from proj_y(C_GA + j * 128)
                        op("dve", "scalar_tensor_tensor", dict(out=u_v(ub, j, 30, 30 + TT), in0=v3(th[:, :TW]), scalar=1.0,
                                                                                 in1=v3(bGa[:, :TW]), op0=ALU.add, op1=ALU.mult), r=[th, bGa], w=[ub])
                        yield
                        op("act", "activation", dict(out=ubf[:, j, :], in_=ub[:, j, :], func=AF.Copy), r=[ub], w=[ubf])
                        yield
                        b2 = bank()
                        def build_cc(k):
                            dslot = dcr[dcc[0] % NSLOT]
                            dcc[0] += 1
                            if ti > 0 and not sample:
                                op("pool", "tensor_scalar", dict(out=dslot[:, :], in0=identb, scalar1=pc(po + P_CC + j * 31 + k), scalar2=0.5,
                                                                 op0=ALU.mult, op1=ALU.mult), r=[cb, prm], w=[dslot])
                            elif dcc[0] % 2:
                                op("act", "activation", dict(out=dslot[:, :], in_=identb, func=AF.Copy, scale=hcc[:, l * 62 + j * 31 + k:l * 62 + j * 31 + k + 1]),
                                   r=[cb, hcc], w=[dslot])
                            else:
                                op("dve", "tensor_scalar", dict(out=dslot[:, :], in0=identb, scalar1=hcc[:, l * 62 + j * 31 + k:l * 62 + j * 31 + k + 1], scalar2=None,
                                                                op0=ALU.mult), r=[cb, hcc], w=[dslot])
                            return dslot
                        slots_ = {}
                        for k in range(CLEAD):
                            slots_[k] = build_cc(k)
                            yield
                        for k in range(31):
                            if k + CLEAD < 31:
                                slots_[k + CLEAD] = build_cc(k + CLEAD)
                                yield
                            dslot = slots_.pop(k)
                            op("pe", "matmul", dict(out=v3(b2[:, :TW]) if sample else b2[:, :TW], lhsT=dslot[:, :],
                                                    rhs=u_v(ubf, j, k, k + TT) if sample else ubf[:, j, k:k + TT],
                                                    start=(k == 0), stop=(k == 30)), r=[dslot, ubf], w=[b2])
                            yield
                        op("act", "activation", dict(out=yc[:, j, :TW], in_=b2[:, :TW], func=AF.Identity, bias=pc(po + P_CB + j)),
                           r=[b2, prm], w=[yc])
                        yield
                        op("pool", "tensor_tensor", dict(out=ysq[j][:, :TW], in0=yc[:, j, :TW], in1=yc[:, j, :TW], op=ALU.mult), r=[yc], w=[ysq[j]])
                        yield
                        if not sample and not lastp:
                            op("pool", "tensor_copy", dict(out=ub[:, j, 0:30], in_=ub[:, j, 512:542]), r=[ub], w=[ub])
                            yield
                    bLS = bank()
                    for s in range(nsub):
                        for j in range(2):
                            op("pe", "matmul", dict(out=bLS[:PT, s * 2:s * 2 + 1], lhsT=yc[:, j, s * 128:s * 128 + PT], rhs=cf[:, F_ONES:F_ONES + 1],
                                                    start=(j == 0), stop=(j == 1)), r=[yc, cf], w=[bLS])
                            yield
                        for j in range(2):
                            op("pe", "matmul", dict(out=bLS[:PT, s * 2 + 1:s * 2 + 2], lhsT=ysq[j][:, s * 128:s * 128 + PT], rhs=cb[:, B_ONES:B_ONES + 1],
                                                    start=(j == 0), stop=(j == 1)), r=[ysq[j], cb], w=[bLS])
                            yield
                    NS2 = slice(0, nsub)
                    op("dve", "tensor_scalar", dict(out=lst[:PT, NS2, :], in0=bLS[:PT, :nsub * 2].rearrange("p (s c) -> p s c", c=2), scalar1=1.0 / 256,
                                                    scalar2=None, op0=ALU.mult), r=[bLS], w=[lst])
                    yield
                    op("dve", "tensor_tensor", dict(out=lvr[:PT, NS2], in0=lst[:PT, NS2, 0], in1=lst[:PT, NS2, 0], op=ALU.mult), r=[lst], w=[lvr])
                    yield
                    op("dve", "tensor_tensor", dict(out=lvr[:PT, NS2], in0=lst[:PT, NS2, 1], in1=lvr[:PT, NS2], op=ALU.subtract), r=[lst, lvr], w=[lvr])
                    yield
                    op("pool", "tensor_scalar", dict(out=lvr[:PT, NS2], in0=lvr[:PT, NS2], scalar1=EPS, scalar2=None, op0=ALU.add), r=[lvr], w=[lvr])
                    yield
                    op("pool", "tensor_tensor", dict(out=LP[:PT, NS2, 0], in0=lvr[:PT, NS2], in1=cf[:PT, F_MHALF:F_MHALF + 1].to_broadcast([PT, nsub]),
                                                     op=ALU.pow), r=[lvr, cf], w=[LP])
                    yield
                    op("dve", "scalar_tensor_tensor", dict(out=LP[:PT, NS2, 1], in0=lst[:PT, NS2, 0], scalar=-1.0, in1=LP[:PT, NS2, 0],
                                                           op0=ALU.mult, op1=ALU.mult), r=[lst, LP], w=[LP])
                    yield
                    bLR = bank()
                    for s in range(nsub):
                        op("pe", "transpose", dict(out=bLR[:2, s * 128:s * 128 + PT], in_=LP[:PT, s, :], identity=identf[:PT, :PT]), r=[LP, cf], w=[bLR])
                        yield
                    op("dve", "tensor_copy", dict(out=LR[:2, :TW], in_=bLR[:2, :TW]), r=[bLR], w=[LR])
                    yield
                    bLA, bLB = bank(), bank()
                    op("pe", "matmul", dict(out=bLA[:, :TW], lhsT=cf[0:2, F_SEL:F_SEL + 128], rhs=LR[0:2, :TW], start=True, stop=True), r=[cf, LR], w=[bLA])
                    yield
                    op("pe", "matmul", dict(out=bLB[:, :TW], lhsT=cf[0:2, F_SEL + 128:F_SEL + 256], rhs=LR[0:2, :TW], start=True, stop=True), r=[cf, LR], w=[bLB])
                    yield
                    for j in range(2):
                        op("dve", "tensor_tensor", dict(out=yc[:, j, :TW], in0=yc[:, j, :TW], in1=bLA[:, :TW], op=ALU.mult), r=[yc, bLA], w=[yc])
                        yield
                        op("dve", "tensor_tensor", dict(out=yc[:, j, :TW], in0=yc[:, j, :TW], in1=bLB[:, :TW], op=ALU.add), r=[yc, bLB], w=[yc])
                        yield
                        thc[0] += 1
                        th = bt["th"] if thc[0] % 2 else bt["th2"]
                        op("act", "activation", dict(out=th[:, :TW], in_=yc[:, j, :TW], func=AF.Tanh, scale=hcl[:, l * 4 + j:l * 4 + j + 1],
                                                     bias=hcl[:, l * 4 + 2 + j:l * 4 + 3 + j]), r=[yc, hcl], w=[th])
                        yield
                        op("act", "activation", dict(out=yc[:, j, :TW], in_=yc[:, j, :TW], func=AF.Identity, scale=pc(po + P_CLG + j),
                                                     bias=pc(po + P_CLB + j)), r=[yc, prm], w=[yc])
                        yield
                        op("dve", "scalar_tensor_tensor", dict(out=yc[:, j, :TW], in0=th[:, :TW], scalar=1.0, in1=yc[:, j, :TW], op0=ALU.add, op1=ALU.mult),
                           r=[th, yc], w=[yc])
                        yield
                        bGc = yield from proj_y(C_GC + j * 128)
                        silu2(bGc, g2[:, :TW], g2)
                        yield
                        op("dve", "scalar_tensor_tensor", dict(out=oT[:, 6 + j, :TW], in0=yc[:, j, :TW], scalar=0.25, in1=g2[:, :TW],
                                                               op0=ALU.mult, op1=ALU.mult), r=[yc, g2], w=[oT])
                        yield
                    if sample and l + 1 < nlayers:
                        load_wgroup(l + 1, 3)
                    if lastp:
                        b = bank()
                        for j in range(2):
                            op("pe", "transpose", dict(out=b[:30, j * 128:(j + 1) * 128], in_=ub[:, j, 512:542], identity=identf),
                               r=[ub, cf], w=[b])
                            yield
                        op("act", "activation", dict(out=stg2[:30, :256], in_=b[:30, :256], func=AF.Copy, scale=0.5), r=[b], w=[stg2])
                        yield
                        op("sp", "dma_start", dict(out=p_cconv[l], in_=stg2[:30, :256]), r=[stg2], dma=True, semkey="stg2")
                        yield
                    if sample:
                        for q in range(4):
                            b = bank()
                            for j in range(2):
                                op("pool", "tensor_copy", dict(out=mu[:, j * 120:(j + 1) * 120].rearrange("p (q t) -> p q t", q=4),
                                                                             in_=u_v(ub, j, 4, 34)[:, 4 * q:4 * q + 4, :]), r=[ub], w=[mu])
                                yield
                                op("pe", "transpose", dict(out=b[:120, j * 128:(j + 1) * 128], in_=mu[:, j * 120:(j + 1) * 120],
                                                                           identity=identf), r=[mu, cf], w=[b])
                                yield
                            op("act", "activation", dict(out=stg[:120, q * 256:(q + 1) * 256], in_=b[:120, :256], func=AF.Copy, scale=0.5),
                               r=[b], w=[stg])
                            yield
                        op("sp", "dma_start", dict(out=s_cconv[l].rearrange("(q p) c -> p q c", p=120),
                                                       in_=stg[:120, :1024].rearrange("p (q c) -> p q c", q=4)), r=[stg], dma=True, semkey="stg")
                        yield

                negS = cb[:PT, (B_NSS if sample else B_NSP):(B_NSS if sample else B_NSP) + PT]
                negI = cb[:PT, (B_NIS if sample else B_NIP):(B_NIS if sample else B_NIP) + PT]
                HW_ = 4 * PT
                sel = [cf[0:3, F_SEL + r_ * 128:F_SEL + r_ * 128 + PT] for r_ in range(3)]
                NG = 1 if sample else NGROUPS

                def hd(t_):
                    return t_[:PT, :].rearrange("p (h j) -> p h j", h=4)

                peng = "dve" if sample else "pool"

                def chunk_gen(s, hs, part):
                    HGn = len(hs)
                    h0 = hs[0]
                    cs_ = slice(s * 128, s * 128 + PT)
                    cP = slice(h0 * PT, (h0 + HGn) * PT)
                    cD = slice(h0 * 128, (h0 + HGn) * 128)
                    qT = {h: qkvT[:, h, cs_] for h in hs}
                    kT = {h: qkvT[:, 4 + h, cs_] for h in hs}
                    vT = {h: qkvT[:, 8 + h, cs_] for h in hs}

                    def g_(X):
                        return X.g(hs)

                    def v3p(ap2):
                        return ap2.rearrange("p (h j) -> p h j", h=HGn)

                    def v3d(ap2):
                        return ap2.rearrange("p (h j) -> p h j", h=HGn)

                    def maskg(col):
                        return cb[:PT, col:col + PT].unsqueeze(1).to_broadcast([PT, HGn, PT])

                    def bcs(t_):
                        return t_[:PT, s, h0:h0 + HGn].unsqueeze(2).to_broadcast([PT, HGn, 128])

                    gcnt = [h0 // max(HGn, 1)]

                    def evac(out_ap, in_ap, rr_, ww_, scale=None):
                        gcnt[0] += 1
                        e_ = "act" if gcnt[0] % 2 else "dve"
                        if e_ == "act":
                            kw_ = dict(out=out_ap, in_=in_ap, func=AF.Copy)
                            if scale is not None:
                                kw_["scale"] = scale
                            op("act", "activation", kw_, r=rr_, w=ww_)
                        elif scale is None:
                            op("dve", "tensor_copy", dict(out=out_ap, in_=in_ap), r=rr_, w=ww_)
                        else:
                            op("dve", "tensor_scalar", dict(out=out_ap, in0=in_ap, scalar1=scale, scalar2=None, op0=ALU.mult), r=rr_, w=ww_)

                    def mm(lt, rt, acc=None):
                        b_ = bank()
                        for h in hs:
                            hh = h - h0
                            if acc is not None:
                                op("pe", "matmul", dict(out=b_[:PT, hh * PT:(hh + 1) * PT], lhsT=identb[:PT, :PT],
                                                        rhs=acc[:PT, h * PT:(h + 1) * PT], start=True, stop=False), r=[acc.t[h], cb], w=[b_])
                            op("pe", "matmul", dict(out=b_[:PT, hh * PT:(hh + 1) * PT], lhsT=lt[:PT, h * PT:(h + 1) * PT],
                                                    rhs=rt[:PT, h * PT:(h + 1) * PT], start=(acc is None), stop=True), r=[lt.t[h], rt.t[h]], w=[b_])
                        return b_

                    Am, Bm, Kw = bt["Am"], bt["Bm"], bt["Kw"]
                    Mt, Ks, Vb = MtB[s % 2], KsB[s % 2], VbB[s % 2]

                    def stages_ad():
                        bR = bank()
                        for h in hs:
                            hh = h - h0
                            op("pe", "transpose", dict(out=bfv(bR)[:9, hh * 128:hh * 128 + PT], in_=R9[:PT, s, h, :], identity=identb[:PT, :PT]),
                               r=[R9, cb], w=[bR])
                        evac(Rw[:9, cD].rearrange("p (h j) -> p h j", h=HGn)[:, :, :PT],
                             bfv(bR)[:9, :HGn * 128].rearrange("p (h j) -> p h j", h=HGn)[:, :, :PT], [bR], g_(Rw))
                        yield
                        bEA, bEM = bank(), bank()
                        for h in hs:
                            hh = h - h0
                            rwh = Rw[0:9, h * 128:h * 128 + PT]
                            o_ = bEA[:PT, hh * PT:(hh + 1) * PT]
                            op("pe", "matmul", dict(out=o_, lhsT=cb[0:9, B_SELNB:B_SELNB + PT], rhs=rwh, start=True, stop=False), r=[Rw.t[h], cb], w=[bEA])
                            op("pe", "matmul", dict(out=o_, lhsT=identb[:PT, :PT], rhs=negS, start=False, stop=True), r=[cb], w=[bEA])
                            o2_ = bEM[:PT, hh * PT:(hh + 1) * PT]
                            op("pe", "matmul", dict(out=o2_, lhsT=cb[0:9, B_SELC:B_SELC + PT], rhs=rwh, start=True, stop=False), r=[Rw.t[h], cb], w=[bEM])
                            op("pe", "matmul", dict(out=o2_, lhsT=identb[:PT, :PT], rhs=negI, start=False, stop=True), r=[cb], w=[bEM])
                        for h in hs:
                            hh = h - h0
                            op("act", "activation", dict(out=EA[:PT, h * PT:(h + 1) * PT], in_=bEA[:PT, hh * PT:(hh + 1) * PT], func=AF.Exp,
                                                         bias=R3[:PT, s, h, 0:1]), r=[bEA, R3], w=[EA.t[h]])
                            op("act", "activation", dict(out=EM[:PT, h * PT:(h + 1) * PT], in_=bEM[:PT, hh * PT:(hh + 1) * PT], func=AF.Exp,
                                                         bias=R3[:PT, s, h, 1:2]), r=[bEM, R3], w=[EM.t[h]])
                        yield
                        bG, bP = bank(), bank()
                        for h in hs:
                            hh = h - h0
                            op("pe", "matmul", dict(out=bG[:PT, hh * PT:(hh + 1) * PT], lhsT=kT[h], rhs=kT[h], start=True, stop=True), r=[qkvT], w=[bG])
                            op("pe", "matmul", dict(out=bP[:PT, hh * PT:(hh + 1) * PT], lhsT=kT[h], rhs=qT[h], start=True, stop=True), r=[qkvT], w=[bP])
                        op("dve", "tensor_tensor", dict(out=Am[:PT, cP], in0=bG[:PT, :HGn * PT], in1=EA[:PT, cP], op=ALU.mult), r=[bG, g_(EA)], w=g_(Am))
                        op("dve", "tensor_tensor", dict(out=Mt[:PT, cP], in0=bP[:PT, :HGn * PT], in1=EM[:PT, cP], op=ALU.mult), r=[bP, g_(EM)], w=g_(Mt))
                        yield
                        bK, bV, bB = bank(), bank(), bank()
                        for h in hs:
                            hh = h - h0
                            op("pe", "transpose", dict(out=bfv(bK)[:PT, hh * 128:(hh + 1) * 128], in_=kT[h], identity=identb), r=[qkvT, cb], w=[bK])
                            op("pe", "transpose", dict(out=bfv(bV)[:PT, hh * 128:(hh + 1) * 128], in_=vT[h], identity=identb), r=[qkvT, cb], w=[bV])
                            op("pe", "transpose", dict(out=bfv(bB)[:PT, hh * PT:(hh + 1) * PT], in_=Am[:PT, h * PT:(h + 1) * PT],
                                                       identity=identb[:PT, :PT]), r=[Am.t[h], cb], w=[bB])
                        kv = v3d(bfv(bK)[:PT, :HGn * 128])
                        vv = v3d(bfv(bV)[:PT, :HGn * 128])
                        if HGn == 1:
                            op("act", "activation", dict(out=Kw[:PT, cD], in_=bfv(bK)[:PT, :128], func=AF.Copy, scale=ea[:PT, s, h0:h0 + 1]), r=[bK, ea], w=g_(Kw))
                            op("dve", "tensor_scalar", dict(out=Ks[:PT, cD], in0=bfv(bK)[:PT, :128], scalar1=es[:PT, s, h0:h0 + 1], scalar2=None, op0=ALU.mult),
                               r=[bK, es], w=g_(Ks))
                            op("act", "activation", dict(out=Vb[:PT, cD], in_=bfv(bV)[:PT, :128], func=AF.Copy, scale=evb[:PT, s, h0:h0 + 1]), r=[bV, evb], w=g_(Vb))
                        else:
                            op("dve", "tensor_tensor", dict(out=v3d(Kw[:PT, cD]), in0=kv, in1=bcs(ea), op=ALU.mult), r=[bK, ea], w=g_(Kw))
                            op("dve", "tensor_tensor", dict(out=v3d(Ks[:PT, cD]), in0=kv, in1=bcs(es), op=ALU.mult), r=[bK, es], w=g_(Ks))
                            op("dve", "tensor_tensor", dict(out=v3d(Vb[:PT, cD]), in0=vv, in1=bcs(evb), op=ALU.mult), r=[bV, evb], w=g_(Vb))
                        evac(Bm[:PT, cP], bfv(bB)[:PT, :HGn * PT], [bB], g_(Bm))
                        yield

                    if part != "main":
                        yield from stages_ad()
                        if part == "pre":
                            return
                    Pc, Qc, Tc, Uc = bt["Pa"], bt["Qa"], bt["Ta"], bt["Ua"]
                    Pn, Qn, Tn, Un = bt["Pb"], bt["Qb"], bt["Tb"], bt["Ub"]
                    op("dve", "tensor_tensor", dict(out=v3p(Pc[:PT, cP]), in0=v3p(Am[:PT, cP]), in1=maskg(B_D16), op=ALU.mult), r=[g_(Am), cb], w=g_(Pc))
                    op(peng, "tensor_tensor", dict(out=v3p(Qc[:PT, cP]), in0=v3p(Bm[:PT, cP]), in1=maskg(B_D16), op=ALU.mult), r=[g_(Bm), cb], w=g_(Qc))
                    op("dve", "tensor_tensor", dict(out=v3p(Tc[:PT, cP]), in0=maskg(B_ID), in1=v3p(Pc[:PT, cP]), op=ALU.subtract), r=[g_(Pc), cb], w=g_(Tc))
                    op(peng, "tensor_tensor", dict(out=v3p(Uc[:PT, cP]), in0=maskg(B_ID), in1=v3p(Qc[:PT, cP]), op=ALU.subtract), r=[g_(Qc), cb], w=g_(Uc))
                    yield
                    for k in range(1, 4):
                        b1 = mm(Qc, Pc)
                        op("act", "activation", dict(out=Pn[:PT, cP], in_=b1[:PT, :HGn * PT], func=AF.Copy), r=[b1], w=g_(Pn))
                        if k < 3:
                            b2 = mm(Pc, Qc)
                            op("act", "activation", dict(out=Qn[:PT, cP], in_=b2[:PT, :HGn * PT], func=AF.Copy), r=[b2], w=g_(Qn))
                        yield
                        b3 = mm(Uc, Pn)
                        op("dve", "tensor_tensor", dict(out=Tn[:PT, cP], in0=b3[:PT, :HGn * PT], in1=Tc[:PT, cP], op=ALU.add), r=[b3, g_(Tc)], w=g_(Tn))
                        b4 = mm(Pn, Uc)
                        op("dve", "tensor_tensor", dict(out=Un[:PT, cP], in0=b4[:PT, :HGn * PT], in1=Uc[:PT, cP], op=ALU.add), r=[b4, g_(Uc)], w=g_(Un))
                        Pc, Pn = Pn, Pc
                        Qc, Qn = Qn, Qc
                        Tc, Tn = Tn, Tc
                        Uc, Un = Un, Uc
                        yield
                    levels = [B_L32, B_L64] + ([] if sample else [B_L128])
                    Ao, Xm = bt["Ao"], bt["Xm"]
                    for li, mcol in enumerate(levels):
                        lastlev = (li == len(levels) - 1)
                        if not lastlev:
                            op(peng, "tensor_tensor", dict(out=v3p(Ao[:PT, cP]), in0=v3p(Bm[:PT, cP]), in1=maskg(mcol), op=ALU.mult), r=[g_(Bm), cb], w=g_(Ao))
                            b1 = mm(Ao, Tc)
                            op("act", "activation", dict(out=Xm[:PT, cP], in_=b1[:PT, :HGn * PT], func=AF.Copy, scale=-1.0), r=[b1], w=g_(Xm))
                            yield
                            b2 = mm(Uc, Xm)
                            op("dve", "tensor_tensor", dict(out=Tn[:PT, cP], in0=b2[:PT, :HGn * PT], in1=Tc[:PT, cP], op=ALU.add), r=[b2, g_(Tc)], w=g_(Tn))
                            yield
                            b3 = bank()
                            for h in hs:
                                hh = h - h0
                                op("pe", "transpose", dict(out=bfv(b3)[:PT, hh * PT:(hh + 1) * PT], in_=Tn[:PT, h * PT:(h + 1) * PT],
                                                           identity=identb[:PT, :PT]), r=[Tn.t[h], cb], w=[b3])
                            evac(Un[:PT, cP], bfv(b3)[:PT, :HGn * PT], [b3], g_(Un))
                            Tc, Tn = Tn, Tc
                            Uc, Un = Un, Uc
                            yield
                        else:
                            op(peng, "tensor_tensor", dict(out=v3p(Ao[:PT, cP]), in0=v3p(Am[:PT, cP]), in1=maskg(mcol), op=ALU.mult), r=[g_(Am), cb], w=g_(Ao))
                            b1 = mm(Ao, Uc)
                            op("act", "activation", dict(out=Xm[:PT, cP], in_=b1[:PT, :HGn * PT], func=AF.Copy, scale=-1.0), r=[b1], w=g_(Xm))
                            yield
                            b2 = mm(Tc, Xm)
                            op("dve", "tensor_tensor", dict(out=Un[:PT, cP], in0=b2[:PT, :HGn * PT], in1=Uc[:PT, cP], op=ALU.add), r=[b2, g_(Uc)], w=g_(Un))
                            Uc, Un = Un, Uc
                            yield
                    U = Uc
                    bW = bank()
                    for h in hs:
                        hh = h - h0
                        op("pe", "matmul", dict(out=bW[:, hh * PT:(hh + 1) * PT], lhsT=Kw[:PT, h * 128:(h + 1) * 128],
                                                rhs=U[:PT, h * PT:(h + 1) * PT], start=True, stop=True), r=[Kw.t[h], U.t[h]], w=[bW])
                    nwT = bt["nwT"]
                    evac(nwT[:, cP], bW[:, :HGn * PT], [bW], g_(nwT), scale=-1.0)
                    yield "chain"
                    vnew = bt["vnew"]
                    bO1 = bank()
                    if not sample:
                        bVn = bank()
                        for h in hs:
                            hh = h - h0
                            op("pe", "matmul", dict(out=bVn[:PT, hh * 128:(hh + 1) * 128], lhsT=U[:PT, h * PT:(h + 1) * PT],
                                                    rhs=Vb[:PT, h * 128:(h + 1) * 128], start=True, stop=False), r=[U.t[h], Vb.t[h]], w=[bVn])
                            op("pe", "matmul", dict(out=bVn[:PT, hh * 128:(hh + 1) * 128], lhsT=nwT[:, h * PT:(h + 1) * PT],
                                                    rhs=Sbf[:, h * 128:(h + 1) * 128], start=False, stop=True), r=[nwT.t[h], Sbf.t[h]], w=[bVn])
                        evac(vnew[:PT, cD], bVn[:PT, :HGn * 128], [bVn], g_(vnew))
                        for h in hs:
                            hh = h - h0
                            op("pe", "matmul", dict(out=bO1[:PT, hh * 128:(hh + 1) * 128], lhsT=qT[h], rhs=Sbf[:, h * 128:(h + 1) * 128],
                                                    start=True, stop=True), r=[qkvT, Sbf.t[h]], w=[bO1])
                    else:
                        bVT, bOT, bST = bank(), bank(), bank()
                        for h in hs:
                            op("pe", "matmul", dict(out=bVT[:, h * PT:(h + 1) * PT], lhsT=Vb[:PT, h * 128:(h + 1) * 128],
                                                    rhs=U[:PT, h * PT:(h + 1) * PT], start=True, stop=True), r=[Vb.t[h], U.t[h]], w=[bVT])
                        S0y = [EA, EM, csb, ys]

                        def ld1_(q):
                            s0_ = S0y[q % 4]
                            op("sp", "dma_start", dict(out=s0_[:, :].rearrange("p (h v) -> p h v", h=4),
                                                       in_=st_delta[l, q].rearrange("h k v -> k h v")), w=[s0_], dma=True, semkey="s0_%d" % (q % 4))
                        for q in range(4):
                            ld1_(q)
                        for q in range(NSEQ_S):
                            s0b = S0b[q % 3]
                            op("act", "activation", dict(out=s0b[:, :], in_=S0y[q % 4][:, :], func=AF.Copy), r=[S0y[q % 4]], w=[s0b])
                            if q + 4 < NSEQ_S:
                                ld1_(q + 4)
                            for h in hs:
                                op("pe", "matmul", dict(out=bST[:, h * PT + q * TS:h * PT + (q + 1) * TS], lhsT=s0b[:, h * 128:(h + 1) * 128],
                                                        rhs=nwT[:, h * PT + q * TS:h * PT + (q + 1) * TS], start=True, stop=True), r=[s0b, nwT], w=[bST])
                                op("pe", "matmul", dict(out=bOT[:, h * PT + q * TS:h * PT + (q + 1) * TS], lhsT=s0b[:, h * 128:(h + 1) * 128],
                                                        rhs=qkvT[:, h, q * TS:(q + 1) * TS], start=True, stop=True), r=[s0b, qkvT], w=[bOT])
                        vnT = bt["vnT"]
                        op("act", "activation", dict(out=t1[:, :HW_], in_=bVT[:, :HW_], func=AF.Copy), r=[bVT], w=[t1])
                        op("dve", "tensor_tensor", dict(out=vnT[:, :HW_], in0=t1[:, :HW_], in1=bST[:, :HW_], op=ALU.add), r=[t1, bST], w=[vnT])
                        op("dve", "tensor_copy", dict(out=o1T[:, :HW_], in_=bOT[:, :HW_]), r=[bOT], w=[o1T])
                        bVn = bank()
                        for h in hs:
                            op("pe", "transpose", dict(out=bfv(bVn)[:PT, h * 128:(h + 1) * 128], in_=vnT[:, h * PT:(h + 1) * PT], identity=identb),
                               r=[vnT, cb], w=[bVn])
                        op("act", "activation", dict(out=vnew[:PT, :], in_=bfv(bVn)[:PT, :512], func=AF.Copy), r=[bVn], w=[vnew])
                        for h in hs:
                            op("pe", "transpose", dict(out=bO1[:PT, h * 128:(h + 1) * 128], in_=o1T[:, h * PT:(h + 1) * PT], identity=identf),
                               r=[o1T, cf], w=[bO1])
                    op("dve", "tensor_tensor", dict(out=v3d(t1[:PT, cD]), in0=v3d(bO1[:PT, :HGn * 128]), in1=bcs(ec), op=ALU.mult), r=[bO1, ec], w=g_(t1))
                    yield
                    bO2 = bank()
                    for h in hs:
                        hh = h - h0
                        op("pe", "matmul", dict(out=bO2[:PT, hh * 128:(hh + 1) * 128], lhsT=Mt[:PT, h * PT:(h + 1) * PT],
                                                rhs=vnew[:PT, h * 128:(h + 1) * 128], start=True, stop=True), r=[Mt.t[h], vnew.t[h]], w=[bO2])
                    op("dve", "tensor_tensor", dict(out=ot[:PT, cD], in0=bO2[:PT, :HGn * 128], in1=t1[:PT, cD], op=ALU.add), r=[bO2, g_(t1)], w=g_(ot))
                    if not sample:
                        bS = bank()
                        for h in hs:
                            hh = h - h0
                            op("pe", "matmul", dict(out=bS[:, hh * 128:(hh + 1) * 128], lhsT=Ks[:PT, h * 128:(h + 1) * 128],
                                                    rhs=vnew[:PT, h * 128:(h + 1) * 128], start=True, stop=True), r=[Ks.t[h], vnew.t[h]], w=[bS])
                        for h in hs:
                            hh = h - h0
                            op("dve", "scalar_tensor_tensor", dict(out=S[:, h * 128:(h + 1) * 128], in0=S[:, h * 128:(h + 1) * 128],
                                                                   scalar=dec[:, s * 4 + h:s * 4 + h + 1], in1=bS[:, hh * 128:(hh + 1) * 128],
                                                                   op0=ALU.mult, op1=ALU.add), r=[S.t[h], dec, bS], w=[S.t[h]])
                        op("act", "activation", dict(out=Sbf[:, cD], in_=S[:, cD], func=AF.Copy), r=g_(S), w=g_(Sbf))
                    else:
                        Ksm = bt["Ksm"]
                        S0x = [EA, EM, csb, ys]
                        SNx = [t1, osq, g2]

                        def ld_(q):
                            s0_ = S0x[q % 4]
                            op("sp", "dma_start", dict(out=s0_[:, :].rearrange("p (h v) -> p h v", h=4),
                                                       in_=st_delta[l, q].rearrange("h k v -> k h v")), w=[s0_], dma=True, semkey="s0_%d" % (q % 4))
                        for q in range(4):
                            ld_(q)
                        for q in range(NSEQ_S):
                            s0, sn = S0x[q % 4], SNx[q % 3]
                            op("act", "activation", dict(out=Ksm[:PT, :], in_=Ks[:PT, :], func=AF.Copy, scale=cf[:PT, F_SEQM + q:F_SEQM + q + 1]),
                               r=[Ks, cf], w=[Ksm])
                            bS = bank()
                            for h in hs:
                                op("pe", "matmul", dict(out=bS[:, h * 128:(h + 1) * 128], lhsT=Ksm[:PT, h * 128:(h + 1) * 128],
                                                        rhs=vnew[:PT, h * 128:(h + 1) * 128], start=True, stop=True), r=[Ksm, vnew], w=[bS])
                            op("dve", "tensor_tensor", dict(out=sn[:, :].rearrange("p (h v) -> p h v", h=4), in0=s0[:, :].rearrange("p (h v) -> p h v", h=4),
                                                            in1=dec[:, q * 4:q * 4 + 4].unsqueeze(2).to_broadcast([128, 4, 128]), op=ALU.mult),
                               r=[s0, dec], w=[sn])
                            op("dve", "tensor_tensor", dict(out=sn[:, :], in0=sn[:, :], in1=bS[:, :], op=ALU.add), r=[sn, bS], w=[sn])
                            op("sp", "dma_start", dict(out=s_delta[l, q].rearrange("h k v -> k h v"), in_=sn[:, :].rearrange("p (h v) -> p h v", h=4)),
                               r=[sn], dma=True, semkey="sn_%d" % (q % 3))
                            if q + 4 < NSEQ_S:
                                ld_(q + 4)
                    yield
                    for h in hs:
                        op("act", "activation", dict(out=osq[:PT, h * 128:(h + 1) * 128], in_=ot[:PT, h * 128:(h + 1) * 128], func=AF.Square,
                                                     accum_out=ssqo[:PT, h:h + 1]), r=[ot.t[h]], w=[osq.t[h], ssqo.t[h]])
                    op("pool", "tensor_scalar", dict(out=rso[:PT, h0:h0 + HGn], in0=ssqo[:PT, h0:h0 + HGn], scalar1=1.0 / 128, scalar2=EPS,
                                                     op0=ALU.mult, op1=ALU.add), r=g_(ssqo), w=g_(rso))
                    op("pool", "tensor_tensor", dict(out=rso[:PT, h0:h0 + HGn], in0=rso[:PT, h0:h0 + HGn],
                                                     in1=cf[:PT, F_MHALF:F_MHALF + 1].to_broadcast([PT, HGn]), op=ALU.pow), r=[g_(rso), cf], w=g_(rso))
                    on = bt["on"]
                    yield
                    op("dve", "tensor_tensor", dict(out=v3d(on[:PT, cD]), in0=v3d(ot[:PT, cD]),
                                                    in1=rso[:PT, h0:h0 + HGn].unsqueeze(2).to_broadcast([PT, HGn, 128]), op=ALU.mult),
                       r=[g_(ot), g_(rso)], w=g_(on))
                    bOT2 = bank()
                    for h in hs:
                        hh = h - h0
                        op("pe", "transpose", dict(out=bfv(bOT2)[:, hh * PT:(hh + 1) * PT], in_=on[:PT, h * 128:(h + 1) * 128],
                                                   identity=identb[:PT, :PT]), r=[on.t[h], cb], w=[bOT2])
                    op("dve", "scalar_tensor_tensor", dict(out=oT[:, h0:h0 + HGn, cs_], in0=v3p(bfv(bOT2)[:, :HGn * PT]),
                                                           scalar=hdng[:, l:l + 1], in1=gd2[:, h0:h0 + HGn, cs_], op0=ALU.mult, op1=ALU.mult),
                       r=[bOT2, hdng, gd2], w=[oT.t[s]])

                xloaded = set()

                def xload_(s2):
                    if s2 in xloaded or s2 >= nsub:
                        return
                    xloaded.add(s2)
                    r2 = tok0 + s2 * 128
                    op("sp", "dma_start", dict(out=xb[s2 % 2][:PT, :], in_=src_x[r2:r2 + PT, :]),
                       r=([y1res[ti * 4 + s2]] if l > 0 else []), w=[xb[s2 % 2]], dma=True, semkey="x%d" % (s2 % 2))

                def outproj_gen(s):
                    r0 = tok0 + s * 128
                    x_ = xb[s % 2]
                    yres = y1res[ti * 4 + s]
                    xload_(s)
                    xload_(s + 1)
                    yield
                    for n in range(2):
                        b = bank()
                        for kc in range(8):
                            op("pe", "matmul", dict(out=b[:PT, :], lhsT=oT[:, kc, s * 128:s * 128 + PT], rhs=Wo[:, kc, n * 512:(n + 1) * 512],
                                                    start=(kc == 0), stop=(kc == 7)), r=[oT.t[s], Wo], w=[b])
                            if kc % 2 == 1:
                                yield
                        op("dve", "tensor_tensor", dict(out=x_[:PT, n * 512:(n + 1) * 512], in0=x_[:PT, n * 512:(n + 1) * 512],
                                                        in1=b[:PT, :], op=ALU.add), r=[x_, b], w=[x_])
                        yield
                    if not last_layer:
                        op("sp", "dma_start", dict(out=y1[r0:r0 + PT, :], in_=x_[:PT, :]), r=[x_], w=[yres], dma=True, semkey="yo%d" % (s % 2))
                    else:
                        ubff = ubf[:, :, :].rearrange("p a b -> p (a b)")
                        op("act", "activation", dict(out=ubff[:PT, 0:D], in_=x_[:PT, :], func=AF.Square, accum_out=ssf[:PT, 0:1]), r=[x_], w=[ubf, ssf])
                        op("act", "activation", dict(out=rsf[:PT, 0:1], in_=ssf[:PT, 0:1], func=AF.Ln, scale=1.0 / D,
                                                     bias=cf[:PT, F_EPS:F_EPS + 1]), r=[ssf, cf], w=[rsf])
                        op("act", "activation", dict(out=rsf[:PT, 0:1], in_=rsf[:PT, 0:1], func=AF.Exp, scale=-0.5), r=[rsf], w=[rsf])
                        yield
                        op("dve", "scalar_tensor_tensor", dict(out=x_[:PT, :], in0=x_[:PT, :], scalar=rsf[:PT, 0:1], in1=prm[:PT, P_FNG:P_FNG + D],
                                                               op0=ALU.mult, op1=ALU.mult), r=[x_, rsf, prm], w=[x_])
                        op("sp", "dma_start", dict(out=yout[r0:r0 + PT, :], in_=x_[:PT, :]), r=[x_], dma=True, semkey="yo%d" % (s % 2))
                    yield

                bg_ = branch_gen()
                e_queue = []
                e_gen = [None]
                f_started = [False]
                f_gen = [None]
                bg_live = [True]
                HG_ = 4 // NG
                grp = [list(range(g * HG_, (g + 1) * HG_)) for g in range(NG)]

                def side_streams(s):
                    if bg_live[0]:
                        bcur[0] = "C"
                        for _ in range(CSTEPS):
                            try:
                                next(bg_)
                            except StopIteration:
                                bg_live[0] = False
                                break
                    elif not sample:
                        bcur[0] = "C"
                        if not f_started[0]:
                            nxt = (l, ti + 1) if ti + 1 < ntiles else ((l + 1, 0) if l + 1 < nlayers else None)
                            f_started[0] = True
                            if nxt is not None:
                                f_gen[0] = stage1_gen(*nxt)
                        if f_gen[0] is not None:
                            try:
                                next(f_gen[0])
                            except StopIteration:
                                f_gen[0] = None
                        for _ in range(ESTEPS):
                            if e_gen[0] is None and e_queue:
                                e_gen[0] = outproj_gen(e_queue.pop(0))
                            if e_gen[0] is None:
                                break
                            try:
                                next(e_gen[0])
                            except StopIteration:
                                e_gen[0] = None

                def run_all(gl):
                    live_ = list(gl)
                    while live_:
                        for gen_ in list(live_):
                            try:
                                next(gen_)
                            except StopIteration:
                                live_.remove(gen_)

                bcur[0] = "D"
                run_all([chunk_gen(0, hs_, "pre") for hs_ in grp])
                for s in range(nsub):
                    mains = [chunk_gen(s, hs_, "main") for hs_ in grp]
                    chain_seen = [False] * len(mains)
                    pres = []
                    pre_started = False
                    live = list(range(len(mains)))
                    while live:
                        bcur[0] = "D"
                        if sample and any(chain_seen) and bg_live[0]:
                            bcur[0] = "C"
                            for _ in bg_:
                                pass
                            bg_live[0] = False
                            bcur[0] = "D"
                        for gi in list(live):
                            try:
                                if next(mains[gi]) == "chain":
                                    chain_seen[gi] = True
                            except StopIteration:
                                live.remove(gi)
                        if (not pre_started) and all(chain_seen) and s + 1 < nsub:
                            pres = [chunk_gen(s + 1, hs_, "pre") for hs_ in grp]
                            pre_started = True
                        for gen_ in list(pres):
                            try:
                                next(gen_)
                            except StopIteration:
                                pres.remove(gen_)
                        side_streams(s)
                    bcur[0] = "D"
                    if s + 1 < nsub and not pre_started:
                        pres = [chunk_gen(s + 1, hs_, "pre") for hs_ in grp]
                    run_all(pres)
                    e_queue.append(s)
                bcur[0] = "C"
                for _ in bg_:
                    pass
                if e_gen[0] is not None:
                    for _ in e_gen[0]:
                        pass
                if f_gen[0] is not None:
                    for _ in f_gen[0]:
                        pass
                bcur[0] = "all"
                if lastp:
                    op("sp", "dma_start", dict(out=p_delta[l].rearrange("h k v -> k h v"), in_=S[:, :].rearrange("p (h v) -> p h v", h=4)),
                       r=[S], dma=True, semkey="pdelta")


                for s_ in e_queue:
                    for _ in outproj_gen(s_):
                        pass
                if sample and l + 1 < nlayers:
                    load_wgroup(l + 1, 4)

        for l in range(nlayers):
            layer(l)
        nsem = P.emit()
        print('instr counts', {e: len(v) for e, v in P.lists.items()}, 'sems', nsem)
    return nc


_NC_CACHE = {}


def kernel(**inp):
    inp = {k: np.asarray(v) for k, v in inp.items()}
    cfc, cbc = make_consts()
    prm = pack_params(inp)
    w_in = inp["w_in"]
    w_in_r = np.ascontiguousarray(np.concatenate([w_in[:, :, 0:1536], w_in[:, :, 1544:3848]], axis=2))
    w_bd = np.ascontiguousarray(w_in[:, :, 1536:1544])
    w_out = np.ascontiguousarray(inp["w_out"])
    if "nc" not in _NC_CACHE:
        _NC_CACHE["nc"] = build_nc()
    nc = _NC_CACHE["nc"]
    in_maps = []
    for b in range(NCORES):
        sl = slice(b * NSEQ_S, (b + 1) * NSEQ_S)
        xin = np.concatenate([inp["x_prompt"][b], inp["x_sample"][sl].reshape(NSEQ_S * TS, D)], axis=0)
        in_maps.append({
            "xin": np.ascontiguousarray(xin, dtype=np.float32),
            "w_in": w_in_r, "w_bd": w_bd, "w_out": w_out, "prm": prm, "cf": cfc, "cb": cbc,
            "st_delta": np.ascontiguousarray(inp["state_delta"][:, sl]),
            "st_qkv": np.ascontiguousarray(inp["state_qkv_conv"][:, sl].reshape(L, NSEQ_S * 3, 1536)),
            "st_sc": np.ascontiguousarray(inp["state_sconv"][:, sl].reshape(L, NSEQ_S * 2, 256)),
            "st_cc": np.ascontiguousarray(inp["state_cconv"][:, sl].reshape(L, NSEQ_S * 30, 256)),
        })
    res = run_bass_kernel_spmd(nc, in_maps, core_ids=list(range(NCORES)))
    R = res.results
    y_prompt = np.stack([R[b]["yout"][:SEQ] for b in range(NCORES)], axis=0)
    y_sample = np.concatenate([R[b]["yout"][SEQ:].reshape(NSEQ_S, TS, D) for b in range(NCORES)], axis=0)
    p_delta = np.stack([R[b]["p_delta"] for b in range(NCORES)], axis=1)
    p_qkv = np.stack([R[b]["p_qkv"] for b in range(NCORES)], axis=1)
    p_sconv = np.stack([R[b]["p_sconv"] for b in range(NCORES)], axis=1)
    p_cconv = np.stack([R[b]["p_cconv"] for b in range(NCORES)], axis=1)
    s_delta = np.concatenate([R[b]["s_delta"] for b in range(NCORES)], axis=1)
    s_qkv = np.concatenate([R[b]["s_qkv"].reshape(L, NSEQ_S, 3, 1536) for b in range(NCORES)], axis=1)
    s_sconv = np.concatenate([R[b]["s_sconv"].reshape(L, NSEQ_S, 2, 256) for b in range(NCORES)], axis=1)
    s_cconv = np.concatenate([R[b]["s_cconv"].reshape(L, NSEQ_S, 30, 256) for b in range(NCORES)], axis=1)
    f = np.float32
    return (y_prompt.astype(f), y_sample.astype(f), p_delta.astype(f), p_qkv.astype(f), p_sconv.astype(f), p_cconv.astype(f),
            s_delta.astype(f), s_qkv.astype(f), s_sconv.astype(f), s_cconv.astype(f))
```
